# Optimizing a Trainium2 kernel written in Bass

```python
import math
import jax, jax.numpy as jnp
from jax import lax
import numpy as np

D_MODEL = 1024
BATCH = 16
SEQ = 2048
DEPTH = 2

HEAD_DIM = 64
A_HEADS = 4
B_HEADS = 6
C_Q_HEADS = 6
C_KV_HEADS = 2
C_GROUP = C_Q_HEADS // C_KV_HEADS
A_W = A_HEADS * HEAD_DIM
B_W = B_HEADS * HEAD_DIM
C_W = C_Q_HEADS * HEAD_DIM
C_KV_W = C_KV_HEADS * HEAD_DIM
Q_BLOCK = 128
BAND_BLOCK = 128
SWA_WINDOW = 128
DILATED_PATTERNS = ((128, 1), (512, 4), (2048, 16))
ROPE_THETA = 10000.0
RMS_EPS = 1e-6
SPLIT_SIZES = (A_W, A_W, A_W, A_HEADS, A_W,
               B_W, B_W, B_W, B_W,
               C_W, C_KV_W, C_KV_W, C_W,
               D_MODEL, D_MODEL, D_MODEL)
IN_COLS = sum(SPLIT_SIZES)

kernel_name = "hybrid_fox_dilated_swa_gated_block"


def rmsnorm(x, g):
    xf = x.astype(jnp.float32)
    y = xf * lax.rsqrt(jnp.mean(xf * xf, axis=-1, keepdims=True) + RMS_EPS)
    return (y * g.astype(jnp.float32)).astype(x.dtype)


def rope_tables(seq):
    pos = jnp.arange(seq, dtype=jnp.float32)
    inv = ROPE_THETA ** (-jnp.arange(0, HEAD_DIM, 2, dtype=jnp.float32) / HEAD_DIM)
    ang = pos[:, None] * inv[None, :]
    return jnp.cos(ang), jnp.sin(ang)


def apply_rope(x, cos, sin):
    xf = x.astype(jnp.float32)
    x1, x2 = xf[..., :HEAD_DIM // 2], xf[..., HEAD_DIM // 2:]
    c, s = cos[None, :, None, :], sin[None, :, None, :]
    return jnp.concatenate([x1 * c - x2 * s, x2 * c + x1 * s], axis=-1).astype(x.dtype)


def banded_attention(q, k, v, max_dist, sink=None):
    n, seq_len, hkv, grp, hd = q.shape
    blk = BAND_BLOCK
    nb = -(-seq_len // blk)
    lp = nb * blk
    pad = lp - seq_len
    qb = jnp.pad(q, ((0, 0), (0, pad), (0, 0), (0, 0), (0, 0))).reshape(n, nb, blk, hkv, grp, hd)

    def key_blocks(t):
        tp = jnp.pad(t, ((0, 0), (blk, pad), (0, 0), (0, 0)))
        prev = tp[:, :lp].reshape(n, nb, blk, hkv, hd)
        cur = tp[:, blk:].reshape(n, nb, blk, hkv, hd)
        return jnp.concatenate([prev, cur], axis=2)

    kb, vb = key_blocks(k), key_blocks(v)
    scale = 1.0 / math.sqrt(hd)
    s = jnp.einsum('nbqhgd,nbkhd->nbhgqk', qb, kb).astype(jnp.float32) * scale
    qi = jnp.arange(blk)[:, None]
    kj = jnp.arange(2 * blk)[None, :]
    dist = qi + blk - kj
    key_pos = (jnp.arange(nb) * blk)[:, None, None] - blk + kj[None]
    valid = (dist >= 0)[None] & (dist <= max_dist)[None] & (key_pos >= 0)
    s = jnp.where(valid[None, :, None, None], s, -jnp.inf)
    m = jnp.max(s, axis=-1, keepdims=True)
    if sink is not None:
        sk = sink.astype(jnp.float32)[None, None, :, :, None, None]
        m = jnp.maximum(m, sk)
    p = jnp.exp(s - m)
    denom = jnp.sum(p, axis=-1, keepdims=True)
    if sink is not None:
        denom = denom + jnp.exp(sk - m)
    lse = (m + jnp.log(denom))[..., 0]
    out = jnp.einsum('nbhgqk,nbkhd->nbqhgd', p / denom, vb.astype(jnp.float32))
    out = out.reshape(n, lp, hkv, grp, hd)[:, :seq_len].astype(q.dtype)
    lse = lse.transpose(0, 1, 4, 2, 3).reshape(n, lp, hkv, grp)[:, :seq_len]
    return out, lse


def forgetting_attention(q, k, v, log_f):
    b, seq, h, hd = q.shape
    nb = seq // Q_BLOCK
    c = jnp.cumsum(log_f, axis=1)
    c_keys = c.transpose(0, 2, 1)[:, :, None, :]
    qb = q.reshape(b, nb, Q_BLOCK, h, hd).transpose(1, 0, 2, 3, 4)
    cqb = c.reshape(b, nb, Q_BLOCK, h).transpose(1, 0, 3, 2)
    vf = v.astype(jnp.float32)
    scale = 1.0 / math.sqrt(hd)
    key_idx = jnp.arange(seq)

    def one_block(args):
        qn, cqn, blk_idx = args
        s = jnp.einsum('bqhd,bkhd->bhqk', qn, k).astype(jnp.float32) * scale
        s = s + cqn[..., None] - c_keys
        qpos = blk_idx * Q_BLOCK + jnp.arange(Q_BLOCK)
        s = jnp.where(key_idx[None, :] <= qpos[:, None], s, -jnp.inf)
        p = jax.nn.softmax(s, axis=-1)
        return jnp.einsum('bhqk,bkhd->bqhd', p, vf)

    out = lax.map(one_block, (qb, cqb, jnp.arange(nb)))
    return out.transpose(1, 0, 2, 3, 4).reshape(b, seq, h, hd).astype(q.dtype)


def dilated_mixture_attention(q, k, v):
    b, seq, h, hd = q.shape
    outs, lses = [], []
    for window, dil in DILATED_PATTERNS:
        sub_len = seq // dil

        def to_sub(t):
            return t.reshape(b, sub_len, dil, h, hd).transpose(0, 2, 1, 3, 4).reshape(b * dil, sub_len, h, hd)

        o, lse = banded_attention(to_sub(q)[:, :, :, None, :], to_sub(k), to_sub(v), window // dil)
        o = o.reshape(b, dil, sub_len, h, hd).transpose(0, 2, 1, 3, 4).reshape(b, seq, h, hd)
        lse = lse.reshape(b, dil, sub_len, h).transpose(0, 2, 1, 3).reshape(b, seq, h)
        outs.append(o.astype(jnp.float32))
        lses.append(lse)
    w = jax.nn.softmax(jnp.stack(lses, axis=0), axis=0)
    out = jnp.sum(w[..., None] * jnp.stack(outs, axis=0), axis=0)
    return out.astype(q.dtype)


def setup_inputs(seed: int = 0) -> dict:
    key = jax.random.key(seed)
    ks = jax.random.split(key, 10)
    f32 = jnp.float32
    x = jax.random.normal(ks[0], (BATCH, SEQ, D_MODEL), f32)
    norm_g = 1.0 + 0.05 * jax.random.normal(ks[1], (DEPTH, D_MODEL), f32)
    w_in = jax.random.normal(ks[2], (DEPTH, D_MODEL, IN_COLS), f32) * D_MODEL ** -0.5
    b_forget = 2.5 + 0.5 * jax.random.normal(ks[3], (DEPTH, A_HEADS), f32)
    sinks = 0.5 * jax.random.normal(ks[4], (DEPTH, C_Q_HEADS), f32)
    w_br_a = jax.random.normal(ks[5], (DEPTH, A_W, D_MODEL), f32) * A_W ** -0.5
    w_br_b = jax.random.normal(ks[6], (DEPTH, B_W, D_MODEL), f32) * B_W ** -0.5
    w_br_c = jax.random.normal(ks[7], (DEPTH, C_W, D_MODEL), f32) * C_W ** -0.5
    w_out = jax.random.normal(ks[8], (DEPTH, D_MODEL, D_MODEL), f32) * D_MODEL ** -0.5
    final_norm_g = 1.0 + 0.05 * jax.random.normal(ks[9], (D_MODEL,), f32)
    return {"x": x, "norm_g": norm_g, "w_in": w_in, "b_forget": b_forget, "sinks": sinks,
            "w_br_a": w_br_a, "w_br_b": w_br_b, "w_br_c": w_br_c, "w_out": w_out,
            "final_norm_g": final_norm_g}


def reference(x, norm_g, w_in, b_forget, sinks, w_br_a, w_br_b, w_br_c, w_out, final_norm_g):
    b, seq, _ = x.shape
    cos, sin = rope_tables(seq)
    offsets = np.cumsum(np.array(SPLIT_SIZES))[:-1].tolist()
    for layer in range(DEPTH):
        h = rmsnorm(x, norm_g[layer])
        u = jnp.einsum('bsd,dc->bsc', h, w_in[layer])
        (qa, ka, va, fa, za,
         qb, kb, vb, zb,
         qc, kc, vc, zc,
         ga, gb, gc) = jnp.split(u, offsets, axis=-1)

        log_f = jax.nn.log_sigmoid(fa.astype(jnp.float32) + b_forget[layer].astype(jnp.float32))
        ya = forgetting_attention(qa.reshape(b, seq, A_HEADS, HEAD_DIM),
                                  ka.reshape(b, seq, A_HEADS, HEAD_DIM),
                                  va.reshape(b, seq, A_HEADS, HEAD_DIM), log_f)
        ya = ya.reshape(b, seq, A_W) * jax.nn.silu(za)

        qb_h = apply_rope(qb.reshape(b, seq, B_HEADS, HEAD_DIM), cos, sin)
        kb_h = apply_rope(kb.reshape(b, seq, B_HEADS, HEAD_DIM), cos, sin)
        yb = dilated_mixture_attention(qb_h, kb_h, vb.reshape(b, seq, B_HEADS, HEAD_DIM))
        yb = yb.reshape(b, seq, B_W) * jax.nn.silu(zb)

        qc_h = apply_rope(qc.reshape(b, seq, C_Q_HEADS, HEAD_DIM), cos, sin)
        qc_h = qc_h.reshape(b, seq, C_KV_HEADS, C_GROUP, HEAD_DIM)
        kc_h = apply_rope(kc.reshape(b, seq, C_KV_HEADS, HEAD_DIM), cos, sin)
        vc_h = vc.reshape(b, seq, C_KV_HEADS, HEAD_DIM)
        yc, _ = banded_attention(qc_h, kc_h, vc_h, SWA_WINDOW - 1,
                                 sinks[layer].reshape(C_KV_HEADS, C_GROUP))
        yc = yc.reshape(b, seq, C_W) * jax.nn.silu(zc)

        merged = (jax.nn.sigmoid(ga) * jnp.einsum('bsc,cd->bsd', ya, w_br_a[layer])
                  + jax.nn.sigmoid(gb) * jnp.einsum('bsc,cd->bsd', yb, w_br_b[layer])
                  + jax.nn.sigmoid(gc) * jnp.einsum('bsc,cd->bsd', yc, w_br_c[layer]))
        x = x + jnp.einsum('bsd,de->bse', merged, w_out[layer])
    return rmsnorm(x, final_norm_g)
```

```python
import numpy as np
from contextlib import ExitStack
import concourse.bass as bass
import concourse.mybir as mybir
from concourse.bass_utils import run_bass_kernel_spmd

F32 = mybir.dt.float32
BF16 = mybir.dt.bfloat16
AF = mybir.ActivationFunctionType
ALU = mybir.AluOpType

T = 2048
NT = 16
NCH = 4
DM = 1024
KC = 8
NCORES = 8
SEQ_PER_CORE = 2
DEPTH = 2
IN_COLS = 6660
O_QA, O_KA, O_VA, O_FA, O_ZA = 0, 256, 512, 768, 772
O_QB, O_KB, O_VB, O_ZB = 1028, 1412, 1796, 2180
O_QC, O_KC, O_VC, O_ZC = 2564, 2948, 3076, 3204
O_GA, O_GB, O_GC = 3588, 4612, 5636
RMS_EPS = 1e-6
HOIST = True
ACT_BIAS_UNITS = ()
FOX_DVE_RECIP = False
M_LE, M_GT, M_B0 = 0, 1, 2
NMASK = 13


class Op:
    __slots__ = ("eng", "fn", "deps", "inc", "count", "dma", "sem", "semval", "semprev", "batch")

    def __init__(self, eng, fn, dma):
        self.eng = eng
        self.fn = fn
        self.dma = dma
        self.deps = []
        self.inc = False
        self.count = 0
        self.sem = None
        self.semval = 0
        self.semprev = 0
        self.batch = None


class Sched:
    ENGS = ("pe", "act", "dve", "pool", "sp")
    N_DMA_SEMS = 24

    def __init__(self):
        self.ops = {e: [] for e in self.ENGS}
        self.lastw = {}
        self.readers = {}
        self.dma_counts = [0] * self.N_DMA_SEMS
        self.dma_rr_q = {"sp": 0, "pool": 0}
        self.nops = 0

    def add(self, eng, fn, reads=(), writes=(), dma=False, batch=None):
        op = Op(eng, fn, dma)
        op.batch = batch
        deps = []
        for r in reads:
            w = self.lastw.get(r)
            if w is not None:
                deps.append((w, "raw"))
        for wr in writes:
            w = self.lastw.get(wr)
            if w is not None:
                deps.append((w, "waw"))
            for rd in self.readers.get(wr, {}).values():
                deps.append((rd, "war"))
        seen = set()
        for p, kind in deps:
            if p is op or id(p) in seen:
                continue
            if (not p.dma) and (not dma) and p.eng == eng:
                if eng == "pe":
                    continue
                if kind == "war":
                    continue
            seen.add(id(p))
            op.deps.append(p)
            if not p.dma:
                p.inc = True
        for r in reads:
            key = eng if not dma else ("dma", id(op))
            self.readers.setdefault(r, {})[key] = op
        for wr in writes:
            self.lastw[wr] = op
            self.readers[wr] = {}
        if dma:
            half = self.N_DMA_SEMS // 2
            base = 0 if eng == "sp" else half
            k = base + self.dma_rr_q[eng] % half
            self.dma_rr_q[eng] += 1
            op.sem = k
            op.semprev = self.dma_counts[k]
            self.dma_counts[k] += 16
            op.semval = self.dma_counts[k]
        self.ops[eng].append(op)
        self.nops += 1
        return op

    def emit(self, nc, block, eng_sems, dma_sems):
        for e in self.ENGS:
            c = 0
            for op in self.ops[e]:
                if (not op.dma) and op.inc:
                    c += 1
                    op.count = c
        handles = {"pe": block.tensor, "act": block.scalar, "dve": block.vector,
                   "pool": block.gpsimd, "sp": block.sync}

        def make(e):
            ops = self.ops[e]

            def body(engh):
                waited = {}

                def wait(key, sem, val):
                    if val <= 0:
                        return
                    if waited.get(key, 0) >= val:
                        return
                    waited[key] = val
                    engh.wait_ge(sem, val)

                def waits_of(op):
                    for p in op.deps:
                        if p.dma:
                            wait(("d", p.sem), dma_sems[p.sem], p.semval)
                        else:
                            wait(("e", p.eng), eng_sems[p.eng], p.count)
                    if op.dma:
                        wait(("d", op.sem), dma_sems[op.sem], op.semprev)

                for idx, op in enumerate(ops):
                    if HOIST and op.batch is not None and (idx == 0 or ops[idx - 1].batch != op.batch):
                        j = idx
                        while j < len(ops) and ops[j].batch == op.batch:
                            waits_of(ops[j])
                            j += 1
                    waits_of(op)
                    if op.fn is None:
                        continue
                    ins = op.fn(engh)
                    if op.dma:
                        ins.then_inc(dma_sems[op.sem], 16)
                    elif op.inc:
                        ins.then_inc(eng_sems[e], 1)
            return body

        for e in self.ENGS:
            handles[e](make(e))


def build_program(n_seq=SEQ_PER_CORE, layers=(0, 1), final_norm=True, stop_after=None, debug=False):
    nc = bass.Bass("TRN2", target_bir_lowering=False)
    S = Sched()

    def din(name, shape, dt=F32):
        return nc.dram_tensor(name, list(shape), dt, kind="ExternalInput").ap()

    x_d = din("x", [n_seq, T, DM])
    w_in_d = din("w_in", [DEPTH, DM, IN_COLS])
    w_bra_d = din("w_br_a", [DEPTH, 256, DM])
    w_brb_d = din("w_br_b", [DEPTH, 384, DM])
    w_brc_d = din("w_br_c", [DEPTH, 384, DM])
    w_out_d = din("w_out", [DEPTH, DM, DM])
    gcol_d = din("gcol", [128, DEPTH * KC])
    bfb_d = din("bfb", [128, DEPTH * 4])
    snk_d = din("snk", [128, DEPTH * 6])
    fgb_d = din("fgb", [128, DM])
    ident_d = din("ident", [128, 128])
    masks_d = din("masks", [128, NMASK * 128])
    trio_d = din("trio", [128, 256])
    cos_d = din("cos_t", [128, NT * 32])
    sin_d = din("sin_t", [128, NT * 32])
    out_d = nc.dram_tensor("out", [n_seq, T, DM], F32, kind="ExternalOutput").ap()
    xs_d = nc.dram_tensor("xs_scratch", [n_seq, T, DM], F32, kind="Internal").ap()
    w_in_b = nc.dram_tensor("w_in_bf", [DEPTH, DM, IN_COLS], BF16, kind="Internal").ap()
    w_bra_b = nc.dram_tensor("w_bra_bf", [DEPTH, 256, DM], BF16, kind="Internal").ap()
    w_brb_b = nc.dram_tensor("w_brb_bf", [DEPTH, 384, DM], BF16, kind="Internal").ap()
    w_brc_b = nc.dram_tensor("w_brc_bf", [DEPTH, 384, DM], BF16, kind="Internal").ap()
    w_out_b = nc.dram_tensor("w_out_bf", [DEPTH, DM, DM], BF16, kind="Internal").ap()
    dbg_d = None
    if debug:
        dbg_d = nc.dram_tensor("dbg", [128, 8 * T], BF16, kind="ExternalOutput").ap()

    es = ExitStack()
    with es:
        def sb(name, shape, dt):
            return es.enter_context(nc.sbuf_tensor("s_" + name, list(shape), dt))

        def ps(name, shape, dt):
            return es.enter_context(nc.psum_tensor("p_" + name, list(shape), dt))

        hT = sb("hT", [128, KC, T], BF16)
        REG = sb("REG", [128, 21504], BF16)
        YT = sb("YT", [128, 8, T], BF16)
        WB_ = [sb("W0", [128, KC * 800], BF16), sb("W1", [128, KC * 800], BF16)]
        WS_ = [sb("WS0", [128, KC, 4, 128], BF16), sb("WS1", [128, KC, 4, 128], BF16)]
        xt = [sb("xt%d" % i, [128, DM], F32) for i in range(3)]
        xn = [sb("xn0", [128, DM], BF16), sb("xn1", [128, DM], BF16)]
        Pb = [sb("P%d" % i, [128, 512], BF16) for i in range(6)]
        t1 = [sb("t1_%d" % i, [128, 384], F32) for i in range(2)]
        t2 = [sb("t2_%d" % i, [128, 384], F32) for i in range(2)]
        rp = [sb("rp_%d" % i, [128, 384], BF16) for i in range(2)]
        sgb = [sb("sg%d" % i, [128, 512], BF16) for i in range(2)]
        acc = sb("acc", [128, 512], F32)
        tmpf = [sb("tmpf%d" % i, [128, 512], F32) for i in range(2)]
        rdn = [sb("rdn%d" % i, [128, 512], F32) for i in range(2)]
        masks = sb("masks", [128, NMASK, 128], BF16)
        ident = sb("ident", [128, 128], BF16)
        trio = sb("trio", [128, 2, 128], F32)
        cos_t = sb("cos_t", [128, NT, 32], F32)
        sin_t = sb("sin_t", [128, NT, 32], F32)
        gcol = sb("gcol", [128, DEPTH, KC], F32)
        bfb = sb("bfb", [128, DEPTH, 4], F32)
        snk = sb("snk", [128, DEPTH * 6], F32)
        esk = sb("esk", [128, DEPTH * 6], F32)
        fgb = sb("fgb", [128, DM], F32)
        biasT = sb("biasT", [128, NT, NT, 4], F32)
        fat = sb("fat", [128, NT, 4], F32)
        spall = sb("spall", [128, NT, 4], F32)
        call = sb("call", [128, NT, 4], F32)
        eall = sb("eall", [128, NT, 4], F32)
        emid = sb("emid", [128, NT, 4], F32)
        stat = sb("stat", [128, 8], F32)

        bankJ = [ps("J0", [128, 512], F32), ps("J1", [128, 512], F32)]
        bankS = [ps("S0", [128, 512], F32), ps("S1", [128, 512], F32)]
        bankO = [ps("O0", [128, 512], F32), ps("O1", [128, 512], F32)]
        bankT = [ps("T0", [128, 1024], BF16), ps("T1", [128, 1024], BF16)]

        ctr = {"J": 0, "T": 0, "G": 0, "O": 0}
        SB4 = [bankS[0], bankS[1], bankJ[0], bankJ[1]]
        SR4 = ["S0", "S1", "J0", "J1"]
        NSB = 6
        SB6 = [bankS[0][:, :], bankS[1][:, :], bankJ[0][:, :], bankJ[1][:, :],
               bankT[0][:, :].bitcast(F32), bankT[1][:, :].bitcast(F32)]
        SR6 = ["S0", "S1", "J0", "J1", "T0", "T1"]

        def nextJ():
            k = ctr["J"] % 2
            ctr["J"] += 1
            return bankJ[k], "J%d" % k

        def nextT():
            k = ctr["T"] % 2
            ctr["T"] += 1
            return bankT[k], "T%d" % k

        def nextG():
            k = ctr["G"] % 4
            ctr["G"] += 1
            return ([bankJ[0], bankJ[1], bankS[0], bankS[1]][k], ["J0", "J1", "S0", "S1"][k])

        def PE(fn, r=(), w=(), batch=None):
            return S.add("pe", fn, r, w, batch=batch)

        def ACT(fn, r=(), w=()):
            return S.add("act", fn, r, w)

        def DVE(fn, r=(), w=()):
            return S.add("dve", fn, r, w)

        def POOL(fn, r=(), w=()):
            return S.add("pool", fn, r, w)

        def DMA(q, out, in_, r=(), w=()):
            return S.add(q, lambda e, o=out, i=in_: e.dma_start(out=o, in_=i), r, w, dma=True)

        def mm(out, lhsT, rhs, start, stop, r, w, batch=None, sgc=False):
            PE(lambda e, o=out, l=lhsT, rr=rhs, s0=start, s1=stop, g_=sgc: e.matmul(
                o, lhsT=l, rhs=rr, start=s0, stop=s1, skip_group_check=g_), r, w, batch=batch)

        def bc(ap, shape):
            return ap.to_broadcast(list(shape))

        DMA("pool", ident[:], ident_d, w=["ident"])
        DMA("pool", masks[:], masks_d.rearrange("p (m c) -> p m c", m=NMASK), w=["masks"])
        DMA("sp", trio[:], trio_d.rearrange("p (m c) -> p m c", m=2), w=["trio"])
        DMA("sp", cos_t[:], cos_d.rearrange("p (t c) -> p t c", t=NT), w=["cos"])
        DMA("sp", sin_t[:], sin_d.rearrange("p (t c) -> p t c", t=NT), w=["sin"])
        DMA("sp", gcol[:], gcol_d.rearrange("p (l k) -> p l k", l=DEPTH), w=["gcol"])
        DMA("sp", bfb[:], bfb_d.rearrange("p (l k) -> p l k", l=DEPTH), w=["bfb"])
        DMA("sp", snk[:], snk_d, w=["snk"])
        DMA("sp", fgb[:], fgb_d, w=["fgb"])
        ACT(lambda e: e.activation(out=esk[:], in_=snk[:], func=AF.Exp), r=["snk"], w=["esk"])

        WGROUPS = [(0, 772), (772, 1028), (1028, 2180), (2180, 2564), (2564, 3204), (3204, 3588), (3588, 6660)]

        def wgroup(c0):
            for gi, (a_, b_) in enumerate(WGROUPS):
                if a_ <= c0 < b_:
                    return gi
            raise ValueError

        for l_ in layers:
            for gi in (0, 2, 4):
                a_, b_ = WGROUPS[gi]
                DMA("pool", w_in_b[l_][:, a_:b_], w_in_d[l_][:, a_:b_], w=[("wdram", l_, gi)])
            DMA("pool", w_out_b[l_], w_out_d[l_], w=[("wdram", l_, "out")])
            for gi in (1, 3, 5):
                a_, b_ = WGROUPS[gi]
                DMA("pool", w_in_b[l_][:, a_:b_], w_in_d[l_][:, a_:b_], w=[("wdram", l_, gi)])
            DMA("pool", w_bra_b[l_], w_bra_d[l_], w=[("wdram", l_, "br")])
            DMA("pool", w_brb_b[l_], w_brb_d[l_], w=[("wdram", l_, "br")])
            DMA("pool", w_brc_b[l_], w_brc_d[l_], w=[("wdram", l_, "br")])
            for q_ in range(4):
                a_ = 3588 + q_ * 768
                DMA("pool", w_in_b[l_][:, a_:a_ + 768], w_in_d[l_][:, a_:a_ + 768], w=[("wdram", l_, 6, q_)])

        def wslot_view(k, ncols):
            return WB_[k][:, 0:KC * ncols].rearrange("p (k c) -> p k c", k=KC)

        def load_w_in(k, l, c0, ncols):
            v = wslot_view(k, ncols)
            src = w_in_b[l].rearrange("(k p) c -> p k c", p=128)[:, :, c0:c0 + ncols]
            DMA("sp", v, src, r=[("wdram", l, wgroup(c0))], w=["W%d" % k])
            return v

        def load_w_out(k, l, half):
            v = wslot_view(k, 512)
            src = w_out_b[l].rearrange("(k p) c -> p k c", p=128)[:, :, half * 512:(half + 1) * 512]
            DMA("sp", v, src, r=[("wdram", l, "out")], w=["W%d" % k])
            return (v, "W%d" % k)

        def ws_dma(l, job):
            if job >= 16:
                return
            slot = job % 2
            wres = "WS%d" % slot
            wv = w_in_b[l].rearrange("(k p) c -> p k c", p=128)
            if job < 8:
                ct = job
                zc = (O_ZA + ct * 128) if ct < 2 else ((O_ZB + (ct - 2) * 128) if ct < 5 else (O_ZC + (ct - 5) * 128))
                DMA("sp", WS_[slot][:, :, 0, :], wv[:, :, zc:zc + 128], r=[("wdram", l, wgroup(zc))], w=[wres])
            else:
                dt = job - 8
                for b_, goff in enumerate((O_GA, O_GB, O_GC)):
                    c0_ = goff + dt * 128
                    DMA("sp", WS_[slot][:, :, b_, :], wv[:, :, c0_:c0_ + 128],
                        r=[("wdram", l, 6, (c0_ - 3588) // 768)], w=[wres])
                DMA("sp", WS_[slot][:, 0:2, 3, :],
                    w_bra_b[l].rearrange("(k p) d -> p k d", p=128)[:, :, dt * 128:(dt + 1) * 128],
                    r=[("wdram", l, "br")], w=[wres])
                DMA("sp", WS_[slot][:, 2:5, 3, :],
                    w_brb_b[l].rearrange("(k p) d -> p k d", p=128)[:, :, dt * 128:(dt + 1) * 128],
                    r=[("wdram", l, "br")], w=[wres])
                DMA("sp", WS_[slot][:, 5:8, 3, :],
                    w_brc_b[l].rearrange("(k p) d -> p k d", p=128)[:, :, dt * 128:(dt + 1) * 128],
                    r=[("wdram", l, "br")], w=[wres])

        def hT_res(c):
            return [("hT", 4 * c + j) for j in range(4)]

        def reg_view(off, shape):
            n = int(np.prod(shape))
            v = REG[:, off:off + n]
            if len(shape) == 1:
                return v
            if len(shape) == 2:
                return v.rearrange("p (a b) -> p a b", a=shape[0])
            if len(shape) == 3:
                return v.rearrange("p (a b c) -> p a b c", a=shape[0], b=shape[1])
            raise ValueError

        def claim_region():
            DVE(lambda e: e.engine_nop(), r=(), w=["REGown"])

        RO = ["REGown"]

        def rms_stats(src_ap, slot, res_src):
            ACT(lambda e, s=slot: e.activation(out=xn[s][:], in_=src_ap, func=AF.Square,
                                               accum_out=stat[:, s:s + 1]),
                r=[res_src], w=["xn%d" % slot, ("stat", slot)])
            ACT(lambda e, s=slot: e.activation(out=stat[:, 2 + s:3 + s], in_=stat[:, s:s + 1], func=AF.Ln,
                                               scale=1.0 / DM, bias=RMS_EPS),
                r=[("stat", slot)], w=[("stat", 2 + slot)])
            ACT(lambda e, s=slot: e.activation(out=stat[:, 4 + s:5 + s], in_=stat[:, 2 + s:3 + s], func=AF.Exp,
                                               scale=-0.5),
                r=[("stat", 2 + slot)], w=[("stat", 4 + slot)])

        def norm_tile(slot, i):
            k = i % 2
            rms_stats(xt[slot][:], k, "xt%d" % slot)
            ACT(lambda e, sl=slot, k=k: e.activation(out=xn[k][:], in_=xt[sl][:], func=AF.Copy,
                                                     scale=stat[:, 4 + k:5 + k]),
                r=["xt%d" % slot, ("stat", 4 + k)], w=["xn%d" % k])

        def transpose_tile(i, l):
            k = i % 2
            tb, tres = nextT()
            for kc in range(KC):
                PE(lambda e, tb=tb, k=k, kc=kc: e.transpose(tb[:, kc * 128:(kc + 1) * 128],
                                                            xn[k][:, kc * 128:(kc + 1) * 128], ident[:, :]),
                   r=["xn%d" % k, "ident"], w=[tres])
            DVE(lambda e, tb=tb, i=i: e.tensor_tensor(
                    out=hT[:, :, i * 128:(i + 1) * 128],
                    in0=tb[:, :].rearrange("p (k c) -> p k c", k=KC),
                    in1=bc(gcol[:, l, :].unsqueeze(2), [128, KC, 128]), op=ALU.mult),
                r=["gcol"], w=[tres, ("hT", i)])

        def norm_transpose_tile(slot, i, l):
            norm_tile(slot, i)
            transpose_tile(i, l)

        def phase0(s, l, xsrc, xsrc_res, after_first_loads=None):
            def ld(i):
                if i < NT:
                    DMA("sp", xt[i % 3][:], xsrc[s, i * 128:(i + 1) * 128, :], r=[(xsrc_res, s, i)],
                        w=["xt%d" % (i % 3)])
            ld(0)
            ld(1)
            if after_first_loads is not None:
                after_first_loads()
            for i in range(NT):
                ld(i + 2)
                norm_transpose_tile(i % 3, i, l)

        def write_V(VV, i, pj, pres, npairs, col0=0):
            src = pj[:, col0:col0 + npairs * 128].rearrange("p (a b c) -> p a b c", a=npairs, b=2)
            ACT(lambda e: e.activation(out=VV[:, i, :, 0:64], in_=src[:, :, 0, :], func=AF.Copy),
                r=RO, w=[pres, ("V", i)])
            ACT(lambda e: e.activation(out=VV[:, i, :, 128:192], in_=src[:, :, 1, :], func=AF.Copy),
                r=RO, w=[pres, ("V", i)])

        def set_ones(VV):
            DVE(lambda e: e.memset(VV[:, :, :, 64:128], 1.0), r=RO, w=[("V", i) for i in range(NT)])

        def rope(pj, pres, nh, i, k, perm=None):
            n = nh * 64
            u = pj[:, 0:n].rearrange("p (h t d) -> p h t d", h=nh, t=2)
            a1 = t1[k][:, 0:n].rearrange("p (h t d) -> p h t d", h=nh, t=2)
            a2 = t2[k][:, 0:n].rearrange("p (h t d) -> p h t d", h=nh, t=2)
            cosb = bc(cos_t[:, i, :].unsqueeze(1).unsqueeze(1), [128, nh, 2, 32])
            sinb = bc(sin_t[:, i, :].unsqueeze(1), [128, nh, 32])
            DVE(lambda e: e.tensor_tensor(out=a1, in0=u, in1=cosb, op=ALU.mult), r=["cos"], w=[pres, ("t1", k)])
            DVE(lambda e: e.tensor_tensor(out=a2[:, :, 0, :], in0=u[:, :, 1, :], in1=sinb, op=ALU.mult),
                r=["sin"], w=[pres, ("t2", k)])
            DVE(lambda e: e.tensor_tensor(out=a2[:, :, 1, :], in0=u[:, :, 0, :], in1=sinb, op=ALU.mult),
                r=["sin"], w=[pres, ("t2", k)])
            if perm is None:
                o = rp[k][:, 0:n].rearrange("p (h t d) -> p h t d", h=nh, t=2)
                o0, o1 = o[:, :, 0, :], o[:, :, 1, :]
                i10, i11 = a1[:, :, 0, :], a1[:, :, 1, :]
                i20, i21 = a2[:, :, 0, :], a2[:, :, 1, :]
            else:
                o = rp[k][:, 0:n].rearrange("p (s v t d) -> p v s t d", s=3, v=2, t=2)
                a1v = t1[k][:, 0:n].rearrange("p (v s t d) -> p v s t d", v=2, s=3, t=2)
                a2v = t2[k][:, 0:n].rearrange("p (v s t d) -> p v s t d", v=2, s=3, t=2)
                o0, o1 = o[:, :, :, 0, :], o[:, :, :, 1, :]
                i10, i11 = a1v[:, :, :, 0, :], a1v[:, :, :, 1, :]
                i20, i21 = a2v[:, :, :, 0, :], a2v[:, :, :, 1, :]
            POOL(lambda e: e.tensor_tensor(out=o0, in0=i10, in1=i20, op=ALU.subtract),
                 r=[("t1", k), ("t2", k)], w=[("rp", k)])
            POOL(lambda e: e.tensor_tensor(out=o1, in0=i11, in1=i21, op=ALU.add),
                 r=[("t1", k), ("t2", k)], w=[("rp", k)])

        def transpose_to(dst3, i, k, ntr, dres):
            tb, tres = nextT()
            for j in range(ntr):
                PE(lambda e, tb=tb, j=j: e.transpose(tb[:, j * 128:(j + 1) * 128], rp[k][:, j * 128:(j + 1) * 128],
                                                     ident[:, :]),
                   r=[("rp", k), "ident"], w=[tres])
            ACT(lambda e, tb=tb: e.activation(out=dst3[:, 0:ntr, i * 128:(i + 1) * 128],
                                              in_=tb[:, 0:ntr * 128].rearrange("p (j c) -> p j c", j=ntr),
                                              func=AF.Copy),
                r=RO, w=[tres, (dres, i)])

        def proj_tok(i, wv, c0, ncols, wres):
            pj, pres = nextG()
            for kc in range(KC):
                mm(pj[:, 0:ncols], hT[:, kc, i * 128:(i + 1) * 128], wv[:, kc, c0:c0 + ncols],
                   kc == 0, kc == KC - 1, [("hT", i), wres], [pres])
            return pj, pres

        OB4 = [bankO[0][:, :], bankO[1][:, :], bankT[0][:, :].bitcast(F32), bankT[1][:, :].bitcast(F32)]
        OR4 = ["O0", "O1", "T0", "T1"]
        pgc = [0]

        def attention(heads, pairing):
            def head_units(hd, slot):
                out = []
                for J in range(NCH):
                    grp = []
                    for i in range(hd["kmin"](J), 4 * J + 4):
                        blks = []
                        for bq in range(4):
                            m = hd["maskfn"](4 * J + bq - i)
                            if m != "skip":
                                blks.append((bq, m))
                        if not blks:
                            continue
                        assert [b_ for b_, _ in blks] == list(range(blks[0][0], blks[0][0] + len(blks)))
                        grp.append((hd, J, i, blks))
                    for n_, u in enumerate(grp):
                        out.append(u + (n_ == 0, n_ == len(grp) - 1, slot, J))
                return out

            units = []
            for (he, ho) in pairing:
                ue, uo = head_units(heads[he], 0), head_units(heads[ho], 1)
                assert len(ue) == len(uo)
                base = pgc[0]
                for x_, y_ in zip(ue, uo):
                    units.append(x_ + (0,))
                    units.append(y_ + (1,))
                pgc[0] += NCH

            def emit_qk(n, batch=None):
                hd, J, i, blks, first, last, slot, _, ob = units[n]
                sbk, sres = SB6[n % 6], SR6[n % 6]
                b0 = blks[0][0]
                ncol = len(blks) * 128
                q0 = J * 512 + b0 * 128
                mm(sbk[:, 0:ncol], hd["kt"](i), hd["qt"](q0, ncol), True, True,
                   RO + [("KT", i)] + [("QT", 4 * J + b_) for b_, _ in blks], [sres], batch=batch)

            def emit_mid(n):
                hd, J, i, blks, first, last, slot, _, ob = units[n]
                sbk, sres = SB6[n % 6], SR6[n % 6]
                pb, pres = Pb[n % 6], "P%d" % (n % 6)
                nb = len(blks)
                ncol = nb * 128
                if hd["bias"] is None:
                    ACT(lambda e: e.activation(out=pb[:, 0:ncol], in_=sbk[:, 0:ncol], func=AF.Exp, scale=0.125),
                        r=[], w=[sres, pres])
                elif ACT_BIAS_UNITS and n % 5 in ACT_BIAS_UNITS:
                    for j, (bq, _) in enumerate(blks):
                        bap = hd["bias"](4 * J + bq, 1, i)
                        ACT(lambda e, j=j, bap=bap: e.activation(out=pb[:, j * 128:(j + 1) * 128],
                                                                 in_=sbk[:, j * 128:(j + 1) * 128],
                                                                 func=AF.Exp, scale=0.125, bias=bap),
                            r=["biasT"], w=[sres, pres])
                else:
                    I0 = 4 * J + blks[0][0]
                    bap = bc(hd["bias"](I0, nb, i).unsqueeze(2), [128, nb, 128])
                    sv = sbk[:, 0:ncol].rearrange("p (b c) -> p b c", b=nb)
                    DVE(lambda e, sv=sv, bap=bap: e.scalar_tensor_tensor(out=sv, in0=sv, scalar=0.125, in1=bap,
                                                                         op0=ALU.mult, op1=ALU.add),
                        r=["biasT"], w=[sres])
                    ACT(lambda e: e.activation(out=pb[:, 0:ncol], in_=sbk[:, 0:ncol], func=AF.Exp),
                        r=[], w=[sres, pres])
                mids = [m for _, m in blks]
                if any(m is not None for m in mids):
                    lo_ = min(j for j in range(nb) if mids[j] is not None)
                    hi_ = max(j for j in range(nb) if mids[j] is not None) + 1
                    sub = mids[lo_:hi_]
                    runs = []
                    if all(m is not None for m in sub) and all(sub[j] == sub[0] + j for j in range(len(sub))):
                        runs.append((lo_, hi_, masks[:, sub[0]:sub[0] + len(sub), :]))
                    elif all(m is not None for m in sub) and all(m == sub[0] for m in sub):
                        runs.append((lo_, hi_, bc(masks[:, sub[0], :].unsqueeze(1), [128, len(sub), 128])))
                    else:
                        j = lo_
                        while j < hi_:
                            m = mids[j]
                            j2 = j
                            while j2 < hi_ and mids[j2] == m:
                                j2 += 1
                            if m is not None:
                                runs.append((j, j2, bc(masks[:, m, :].unsqueeze(1), [128, j2 - j, 128])))
                            j = j2
                    for (ja, jb, mk) in runs:
                        view = pb[:, ja * 128:jb * 128].rearrange("p (b c) -> p b c", b=jb - ja)
                        DVE(lambda e, view=view, mk=mk: e.tensor_tensor(out=view, in0=view, in1=mk, op=ALU.mult),
                            r=["masks"], w=[pres])

            def emit_pv(n, batch=None):
                hd, J, i, blks, first, last, slot, _, ob = units[n]
                pb, pres = Pb[n % 6], "P%d" % (n % 6)
                b0 = blks[0][0]
                ncol = len(blks) * 128
                obk, ores = OB4[ob], OR4[ob]
                mm(obk[:, b0 * 128:b0 * 128 + ncol], hd["v"](i), pb[:, 0:ncol], first, last,
                   RO + [pres, ("V", i)], [ores], batch=batch, sgc=True)
                if last:
                    rd = rdn[J % 2]
                    nr, dr, orr = hd["num_rows"], hd["den_rows"], hd["out_rows"]
                    rres = ("rdn", J % 2, orr.start)
                    if hd.get("dve_recip"):
                        DVE(lambda e: e.reciprocal(out=rd[orr, :], in_=obk[dr, :]), r=[], w=[ores, rres])
                    else:
                        if hd["sink"] is not None:
                            sc = hd["sink"]
                            ACT(lambda e: e.activation(out=rd[orr, :], in_=obk[dr, :], func=AF.Ln,
                                                       bias=esk[dr, sc:sc + 1]),
                                r=["esk"], w=[ores, rres])
                        else:
                            ACT(lambda e: e.activation(out=rd[orr, :], in_=obk[dr, :], func=AF.Ln),
                                r=[], w=[ores, rres])
                        ACT(lambda e: e.activation(out=rd[orr, :], in_=rd[orr, :], func=AF.Exp, scale=-1.0),
                            r=[], w=[rres])
                    ct = hd["ct"]
                    DVE(lambda e: e.tensor_tensor(out=YT[orr, ct, J * 512:(J + 1) * 512], in0=obk[nr, :],
                                                  in1=rd[orr, :], op=ALU.mult),
                        r=[rres], w=[ores, ("YT", ct, J)])

            LOOK, GB = 4, 2
            nu = len(units)
            qn = 0
            for m in range(0, nu, GB):
                tag = ("qk", id(units), m)
                while qn < min(nu, m + GB + LOOK):
                    emit_qk(qn, batch=tag)
                    qn += 1
                for n in range(m, min(nu, m + GB)):
                    emit_mid(n)
                tag = ("pv", id(units), m)
                for n in range(m, min(nu, m + GB)):
                    emit_pv(n, batch=tag)

        LO, HI = slice(0, 64), slice(64, 128)

        def layer(s, l, xsrc, xsrc_res, last_layer, skip_p0=False, next_l=None):
            wpre = {}

            def wprefetch():
                wpre["wa"] = load_w_in(1, l, O_QA, 772)
                wpre["wb1"] = load_w_in(0, l, O_QB, 768)
            if not skip_p0:
                phase0(s, l, xsrc, xsrc_res, after_first_loads=wprefetch)
            else:
                wprefetch()
            wa, wb1 = wpre["wa"], wpre["wb1"]

            claim_region()
            QTa = reg_view(0, [2, T])
            KTa = reg_view(2 * T, [2, T])
            VA = reg_view(4 * T, [NT, 2, 192])
            set_ones(VA)
            for ctile in range(4):
                dst = QTa if ctile < 2 else KTa
                dres = "QT" if ctile < 2 else "KT"
                for c in range(NCH):
                    pj, pres = nextG()
                    for kc in range(KC):
                        mm(pj[:, :], wa[:, kc, ctile * 128:(ctile + 1) * 128], hT[:, kc, c * 512:(c + 1) * 512],
                           kc == 0, kc == KC - 1, hT_res(c) + ["W1"], [pres])
                    ACT(lambda e, pj=pj, dst=dst, ctile=ctile, c=c: e.activation(
                            out=dst[:, ctile % 2, c * 512:(c + 1) * 512], in_=pj[:, :], func=AF.Copy),
                        r=RO, w=[pres] + [(dres, 4 * c + j) for j in range(4)])
            for i in range(NT):
                pj, pres = proj_tok(i, wa, 512, 260, "W1")
                write_V(VA, i, pj, pres, 2)
                DVE(lambda e, pj=pj, i=i: e.tensor_tensor(out=fat[:, i, :], in0=pj[:, 256:260], in1=bfb[:, l, :],
                                                          op=ALU.add),
                    r=["bfb"], w=[pres, "fat"])
            wb2 = load_w_in(1, l, O_VB, 384)
            ACT(lambda e: e.activation(out=spall[:], in_=fat[:], func=AF.Exp, scale=-1.0), r=["fat"], w=["spall"])
            ACT(lambda e: e.activation(out=spall[:], in_=spall[:], func=AF.Ln, bias=1.0), r=["spall"], w=["spall"])
            pj, pres = nextJ()
            spf = spall[:].rearrange("p t h -> p (t h)")
            mm(pj[:, 0:64], trio[:, 0, :], spf, True, True, ["trio", "spall"], [pres])
            mm(pj[:, 64:128], trio[:, 1, :], spf, True, True, ["trio", "spall"], [pres])
            cb = pj[:, 0:64].rearrange("p (t h) -> p t h", t=NT)
            tot = pj[:, 64:128].rearrange("p (t h) -> p t h", t=NT)
            DVE(lambda e: e.tensor_copy(out=eall[:, 0, :], in_=tot[:, 0, :]), r=[], w=[pres, ("eall", 0)])
            for i in range(1, NT):
                DVE(lambda e, i=i: e.tensor_tensor(out=eall[:, i, :], in0=tot[:, i, :], in1=eall[:, i - 1, :],
                                                   op=ALU.add),
                    r=[("eall", i - 1)], w=[pres, ("eall", i)])
            DVE(lambda e: e.tensor_copy(out=call[:, 0, :], in_=cb[:, 0, :]), r=[], w=[pres, "call"])
            DVE(lambda e: e.tensor_tensor(out=call[:, 1:NT, :], in0=cb[:, 1:NT, :], in1=eall[:, 0:NT - 1, :],
                                          op=ALU.add),
                r=[("eall", i) for i in range(NT)], w=[pres, "call"])
            DVE(lambda e: e.tensor_scalar(out=emid[:, 0, :], in0=eall[:, 0, :], scalar1=0.5, scalar2=None,
                                          op0=ALU.mult),
                r=[("eall", i) for i in range(NT)], w=["emid"])
            DVE(lambda e: e.tensor_tensor(out=emid[:, 1:NT, :], in0=eall[:, 1:NT, :], in1=eall[:, 0:NT - 1, :],
                                          op=ALU.add),
                r=[("eall", i) for i in range(NT)], w=["emid"])
            DVE(lambda e: e.tensor_scalar(out=emid[:, 1:NT, :], in0=emid[:, 1:NT, :], scalar1=0.5, scalar2=None,
                                          op0=ALU.mult),
                r=["emid"], w=["emid"])
            for I in range(NT):
                DVE(lambda e, I=I: e.tensor_tensor(out=biasT[:, I, 0:I + 1, :], in0=call[:, 0:I + 1, :],
                                                   in1=bc(emid[:, I, :].unsqueeze(1), [128, I + 1, 4]),
                                                   op=ALU.subtract),
                    r=["call", "emid"], w=["biasT"])
            heads = []
            for h in range(4):
                p_, hf = h // 2, h % 2
                rows = LO if hf == 0 else HI
                heads.append(dict(
                    qt=lambda q0, n, p_=p_, rows=rows: QTa[rows, p_, q0:q0 + n],
                    kt=lambda i, p_=p_, rows=rows: KTa[rows, p_, i * 128:(i + 1) * 128],
                    v=lambda i, p_=p_, hf=hf: VA[:, i, p_, (0 if hf == 0 else 64):(128 if hf == 0 else 192)],
                    num_rows=rows, den_rows=(HI if hf == 0 else LO), out_rows=rows, ct=p_,
                    maskfn=lambda d: ("skip" if d < 0 else (M_LE if d == 0 else None)),
                    bias=lambda I0, nb, i, h=h: biasT[:, I0:I0 + nb, i, h], sink=None, kmin=lambda J: 0, dve_recip=FOX_DVE_RECIP))
            attention(heads, [(0, 1), (2, 3)])
            if stop_after == "A":
                return

            claim_region()
            QTb = reg_view(0, [3, T])
            KTb = reg_view(3 * T, [3, T])
            VB = reg_view(6 * T, [NT, 3, 192])
            set_ones(VB)
            pend = []

            def flush():
                for a_ in pend:
                    transpose_to(*a_)
                del pend[:]
            for i in range(NT):
                pjq, prq = proj_tok(i, wb1, 0, 384, "W0")
                pjk, prk = proj_tok(i, wb1, 384, 384, "W0")
                pjv, prv = proj_tok(i, wb2, 0, 384, "W1")
                flush()
                rope(pjq, prq, 6, i, 0)
                pend.append((QTb, i, 0, 3, "QT"))
                rope(pjk, prk, 6, i, 1)
                pend.append((KTb, i, 1, 3, "KT"))
                write_V(VB, i, pjv, prv, 3)
            flush()
            wc = load_w_in(0, l, O_QC, 640)
            wo = [load_w_out(1, l, 0)]

            def maskB(d):
                if d < 0:
                    return "skip"
                return M_B0 + min(d, 10)
            heads = []
            for h in range(6):
                p_, hf = h // 2, h % 2
                rows = LO if hf == 0 else HI
                heads.append(dict(
                    qt=lambda q0, n, p_=p_, rows=rows: QTb[rows, p_, q0:q0 + n],
                    kt=lambda i, p_=p_, rows=rows: KTb[rows, p_, i * 128:(i + 1) * 128],
                    v=lambda i, p_=p_, hf=hf: VB[:, i, p_, (0 if hf == 0 else 64):(128 if hf == 0 else 192)],
                    num_rows=rows, den_rows=(HI if hf == 0 else LO), out_rows=rows, ct=2 + p_,
                    maskfn=maskB, bias=None, sink=None, kmin=lambda J: 0))
            attention(heads, [(0, 1), (2, 3), (4, 5)])
            if stop_after == "B":
                return

            claim_region()
            QTc = reg_view(0, [3, T])
            KTc = reg_view(3 * T, [1, T])
            VC = reg_view(4 * T, [NT, 1, 192])
            set_ones(VC)
            for i in range(NT):
                pjq, prq = proj_tok(i, wc, 0, 384, "W0")
                pjk, prk = proj_tok(i, wc, 384, 256, "W0")
                flush()
                rope(pjq, prq, 6, i, 0, perm=True)
                pend.append((QTc, i, 0, 3, "QT"))
                rope(pjk, prk, 2, i, 1)
                pend.append((KTc, i, 1, 1, "KT"))
                write_V(VC, i, pjk, prk, 1, col0=128)
            flush()
            wo.append(load_w_out(0, l, 1))
            ws_dma(l, 0)
            heads = []
            for g in range(6):
                kv, sl = g // 3, g % 3
                rows = LO if kv == 0 else HI
                heads.append(dict(
                    qt=lambda q0, n, sl=sl, rows=rows: QTc[rows, sl, q0:q0 + n],
                    kt=lambda i, rows=rows: KTc[rows, 0, i * 128:(i + 1) * 128],
                    v=lambda i, kv=kv: VC[:, i, 0, (0 if kv == 0 else 64):(128 if kv == 0 else 192)],
                    num_rows=rows, den_rows=(HI if kv == 0 else LO), out_rows=(LO if g % 2 == 0 else HI),
                    ct=5 + g // 2,
                    maskfn=lambda d: (M_LE if d == 0 else (M_GT if d == 1 else "skip")),
                    bias=None, sink=l * 6 + g, kmin=lambda J: max(0, 4 * J - 1)))
            attention(heads, [(0, 3), (1, 4), (2, 5)])
            if stop_after == "C":
                return

            n3 = 0
            for ct in range(8):
                slot = ct % 2
                ws_dma(l, ct + 1)
                for c in range(NCH):
                    pj, pres = nextG()
                    for kc in range(KC):
                        mm(pj[:, :], WS_[slot][:, kc, 0, :], hT[:, kc, c * 512:(c + 1) * 512],
                           kc == 0, kc == KC - 1, hT_res(c) + ["WS%d" % slot], [pres])
                    k = n3 % 2
                    n3 += 1
                    ACT(lambda e, pj=pj, k=k: e.activation(out=sgb[k][:], in_=pj[:, :], func=AF.Silu),
                        r=[], w=[pres, ("sg", k)])
                    POOL(lambda e, ct=ct, c=c, k=k: e.tensor_tensor(out=YT[:, ct, c * 512:(c + 1) * 512],
                                                                    in0=YT[:, ct, c * 512:(c + 1) * 512],
                                                                    in1=sgb[k][:], op=ALU.mult),
                         r=[("sg", k)], w=[("YT", ct, c)])
            if stop_after == "P3a":
                return

            claim_region()
            MT = reg_view(0, [8, T])
            br_ct = [(0, 2), (2, 5), (5, 8)]
            for dt in range(8):
                slot = dt % 2
                wres = "WS%d" % slot
                ws_dma(l, 8 + dt + 1)
                for c in range(NCH):
                    for b_ in range(3):
                        gb_, gres = nextG()
                        for kc in range(KC):
                            mm(gb_[:, :], WS_[slot][:, kc, b_, :], hT[:, kc, c * 512:(c + 1) * 512],
                               kc == 0, kc == KC - 1, hT_res(c) + [wres], [gres])
                        k = n3 % 2
                        n3 += 1
                        ACT(lambda e, gb_=gb_, k=k: e.activation(out=sgb[k][:], in_=gb_[:, :], func=AF.Sigmoid),
                            r=[], w=[gres, ("sg", k)])
                        pb_, pbres = nextG()
                        c0, c1 = br_ct[b_]
                        for ct in range(c0, c1):
                            mm(pb_[:, :], WS_[slot][:, ct, 3, :], YT[:, ct, c * 512:(c + 1) * 512],
                               ct == c0, ct == c1 - 1, [("YT", ct, c), wres], [pbres])
                        if b_ == 0:
                            DVE(lambda e, pb_=pb_, k=k: e.tensor_tensor(out=acc[:], in0=pb_[:, :], in1=sgb[k][:],
                                                                        op=ALU.mult),
                                r=[("sg", k)], w=[pbres, "acc"])
                        else:
                            tk = b_ - 1
                            DVE(lambda e, pb_=pb_, k=k, tk=tk: e.tensor_tensor(out=tmpf[tk][:], in0=pb_[:, :],
                                                                               in1=sgb[k][:], op=ALU.mult),
                                r=[("sg", k)], w=[pbres, ("tmpf", tk)])
                            if b_ == 1:
                                POOL(lambda e, tk=tk: e.tensor_tensor(out=acc[:], in0=acc[:], in1=tmpf[tk][:],
                                                                      op=ALU.add),
                                     r=[("tmpf", tk)], w=["acc"])
                            else:
                                POOL(lambda e, tk=tk, dt=dt, c=c: e.tensor_tensor(
                                        out=MT[:, dt, c * 512:(c + 1) * 512], in0=acc[:], in1=tmpf[tk][:],
                                        op=ALU.add),
                                     r=RO + [("tmpf", tk), "acc"], w=[("MT", dt, c)])
            if stop_after == "P3b":
                return

            def ld3(i):
                if i < NT:
                    DMA("sp", xt[i % 3][:], xsrc[s, i * 128:(i + 1) * 128, :], r=[(xsrc_res, s, i)],
                        w=["xt%d" % (i % 3)])
            for i in range(NT):
                slot = i % 3
                k = i % 2
                xres = "xt%d" % slot
                if i == 0:
                    ld3(0)
                    ld3(1)
                ld3(i + 2)
                for half in range(2):
                    pj, pres = nextG()
                    wv, wres = wo[half]
                    for dc in range(KC):
                        mm(pj[:, :], MT[:, dc, i * 128:(i + 1) * 128], wv[:, dc, :], dc == 0, dc == KC - 1,
                           RO + [("MT", dc, i // 4), wres], [pres])
                    DVE(lambda e, pj=pj, slot=slot, half=half: e.tensor_tensor(
                            out=xt[slot][:, half * 512:(half + 1) * 512], in0=pj[:, :],
                            in1=xt[slot][:, half * 512:(half + 1) * 512], op=ALU.add),
                        r=[], w=[pres, xres])
                if not last_layer:
                    DMA("sp", xs_d[s, i * 128:(i + 1) * 128, :], xt[slot][:], r=[xres], w=[("xs", s, i)])
                    if next_l is not None:
                        norm_tile(slot, i)
                        if i > 0:
                            transpose_tile(i - 1, next_l)
                        if i == NT - 1:
                            transpose_tile(i, next_l)
                else:
                    if final_norm:
                        rms_stats(xt[slot][:], k, xres)
                        DVE(lambda e, slot=slot, k=k: e.scalar_tensor_tensor(
                                out=xt[slot][:], in0=xt[slot][:], scalar=stat[:, 4 + k:5 + k], in1=fgb[:],
                                op0=ALU.mult, op1=ALU.mult),
                            r=[("stat", 4 + k), "fgb"], w=[xres])
                    DMA("sp", out_d[s, i * 128:(i + 1) * 128, :], xt[slot][:], r=[xres], w=[("out", s, i)])

        for s in range(n_seq):
            for li, l in enumerate(layers):
                first = li == 0
                lastl = li == len(layers) - 1
                layer(s, l, x_d if first else xs_d, "xin" if first else "xs", lastl,
                      skip_p0=(not first) and stop_after is None,
                      next_l=(None if (lastl or stop_after is not None) else layers[li + 1]))

        if debug:
            DMA("sp", dbg_d, YT[:].rearrange("p a b -> p (a b)"),
                r=[("YT", ct, c) for ct in range(8) for c in range(NCH)], w=["dbg"])
        fin_reads = ["dbg"] if debug else []
        for s in range(n_seq):
            for i in range(NT):
                fin_reads.append(("out", s, i))
                fin_reads.append(("xs", s, i))
        S.add("sp", None, fin_reads, [])

        eng_sems = {}
        for e in Sched.ENGS:
            eng_sems[e] = es.enter_context(nc.semaphore("sem_" + e))
        dma_sems = [es.enter_context(nc.semaphore("dsem%d" % k)) for k in range(Sched.N_DMA_SEMS)]
        block = es.enter_context(nc.Block())
        S.emit(nc, block, eng_sems, dma_sems)
    return nc, S


def host_constants():
    k = np.arange(128)[:, None]
    q = np.arange(128)[None, :]
    le = (k <= q).astype(np.float32)
    ge = (k >= q).astype(np.float32)
    gt = (k > q).astype(np.float32)
    m4 = (((q - k) % 4) == 0).astype(np.float32)
    m16 = (((q - k) % 16) == 0).astype(np.float32)
    b5 = m16
    masks = np.stack([le, gt, le * (1 + m4 + m16), ge + m4 + m16, m4 + m16, m4 + m16, ge * m4 + m16,
                      b5, b5, b5, b5, b5, b5], axis=1)
    assert masks.shape[1] == NMASK
    masks = np.ascontiguousarray(masks.reshape(128, NMASK * 128).astype(np.float32))
    ident = np.eye(128, dtype=np.float32)
    tri = (k <= q).astype(np.float32)
    ones = np.ones((128, 128), np.float32)
    trio = np.ascontiguousarray(np.concatenate([tri, ones], axis=1))
    pos = np.arange(T, dtype=np.float32)
    inv = (np.float32(10000.0) ** (-(np.arange(0, 64, 2, dtype=np.float32)) / np.float32(64))).astype(np.float32)
    ang = (pos[:, None] * inv[None, :]).astype(np.float32)
    cos = np.cos(ang).astype(np.float32).reshape(NT, 128, 32).transpose(1, 0, 2).reshape(128, NT * 32)
    sin = np.sin(ang).astype(np.float32).reshape(NT, 128, 32).transpose(1, 0, 2).reshape(128, NT * 32)
    return dict(ident=ident, masks=masks, trio=trio, cos_t=np.ascontiguousarray(cos), sin_t=np.ascontiguousarray(sin))


def make_in_maps(x_shards, norm_g, w_in, b_forget, sinks, w_br_a, w_br_b, w_br_c, w_out, final_norm_g):
    consts = host_constants()
    f = lambda a: np.ascontiguousarray(np.asarray(a, dtype=np.float32))
    gcol = f(np.asarray(norm_g).reshape(DEPTH, KC, 128).transpose(2, 0, 1).reshape(128, DEPTH * KC))
    bfb = f(np.broadcast_to(np.asarray(b_forget).reshape(1, DEPTH * 4), (128, DEPTH * 4)))
    snk = f(np.broadcast_to(np.asarray(sinks).reshape(1, DEPTH * 6), (128, DEPTH * 6)))
    fgb = f(np.broadcast_to(np.asarray(final_norm_g).reshape(1, DM), (128, DM)))
    shared = dict(w_in=f(w_in), w_br_a=f(w_br_a), w_br_b=f(w_br_b), w_br_c=f(w_br_c), w_out=f(w_out),
                  gcol=gcol, bfb=bfb, snk=snk, fgb=fgb, **consts)
    return [dict(x=f(xs), **shared) for xs in x_shards]


_PROGRAM = None


def kernel(x, norm_g, w_in, b_forget, sinks, w_br_a, w_br_b, w_br_c, w_out, final_norm_g):
    global _PROGRAM
    x = np.asarray(x, dtype=np.float32)
    shards = [x[c * SEQ_PER_CORE:(c + 1) * SEQ_PER_CORE] for c in range(NCORES)]
    if _PROGRAM is None:
        _PROGRAM = build_program()[0]
    in_maps = make_in_maps(shards, norm_g, w_in, b_forget, sinks, w_br_a, w_br_b, w_br_c, w_out, final_norm_g)
    res = run_bass_kernel_spmd(_PROGRAM, in_maps, core_ids=list(range(NCORES)))
    out = np.concatenate([np.asarray(r["out"], dtype=np.float32) for r in res.results], axis=0)
    return out
```

```python
import numpy as np
from contextlib import ExitStack
import concourse.bass as bass
import concourse.mybir as mybir
from concourse.bass_utils import run_bass_kernel_spmd

F32 = mybir.dt.float32
BF16 = mybir.dt.bfloat16
AF = mybir.ActivationFunctionType
ALU = mybir.AluOpType

T = 2048
NT = 16
NCH = 4
DM = 1024
KC = 8
NCORES = 8
SEQ_PER_CORE = 2
DEPTH = 2
IN_COLS = 6660
O_QA, O_KA, O_VA, O_FA, O_ZA = 0, 256, 512, 768, 772
O_QB, O_KB, O_VB, O_ZB = 1028, 1412, 1796, 2180
O_QC, O_KC, O_VC, O_ZC = 2564, 2948, 3076, 3204
O_GA, O_GB, O_GC = 3588, 4612, 5636
RMS_EPS = 1e-6
HOIST = True
POOL_DMA_INFLIGHT = 3
ACT_BIAS_UNITS = ()
FOX_DVE_RECIP = False
M_LE, M_GT, M_B0 = 0, 1, 2
NMASK = 13


class Op:
    __slots__ = ("eng", "fn", "deps", "inc", "count", "dma", "sem", "semval", "semprev", "batch")

    def __init__(self, eng, fn, dma):
        self.eng = eng
        self.fn = fn
        self.dma = dma
        self.deps = []
        self.inc = False
        self.count = 0
        self.sem = None
        self.semval = 0
        self.semprev = 0
        self.batch = None


class Sched:
    ENGS = ("pe", "act", "dve", "pool", "sp")
    N_DMA_SEMS = 24

    def __init__(self):
        self.ops = {e: [] for e in self.ENGS}
        self.lastw = {}
        self.readers = {}
        self.dma_counts = [0] * self.N_DMA_SEMS
        self.dma_rr_q = {"sp": 0, "pool": 0}
        self.nops = 0

    def add(self, eng, fn, reads=(), writes=(), dma=False, batch=None):
        op = Op(eng, fn, dma)
        op.batch = batch
        deps = []
        for r in reads:
            w = self.lastw.get(r)
            if w is not None:
                deps.append((w, "raw"))
        for wr in writes:
            w = self.lastw.get(wr)
            if w is not None:
                deps.append((w, "waw"))
            for rd in self.readers.get(wr, {}).values():
                deps.append((rd, "war"))
        seen = set()
        for p, kind in deps:
            if p is op or id(p) in seen:
                continue
            if (not p.dma) and (not dma) and p.eng == eng:
                if eng == "pe":
                    continue
                if kind == "war":
                    continue
            seen.add(id(p))
            op.deps.append(p)
            if not p.dma:
                p.inc = True
        for r in reads:
            key = eng if not dma else ("dma", id(op))
            self.readers.setdefault(r, {})[key] = op
        for wr in writes:
            self.lastw[wr] = op
            self.readers[wr] = {}
        if dma:
            half = self.N_DMA_SEMS // 2
            base = 0 if eng == "sp" else half
            nq = half if eng == "sp" else POOL_DMA_INFLIGHT
            k = base + self.dma_rr_q[eng] % nq
            self.dma_rr_q[eng] += 1
            op.sem = k
            op.semprev = self.dma_counts[k]
            self.dma_counts[k] += 16
            op.semval = self.dma_counts[k]
        self.ops[eng].append(op)
        self.nops += 1
        return op

    def emit(self, nc, block, eng_sems, dma_sems):
        for e in self.ENGS:
            c = 0
            for op in self.ops[e]:
                if (not op.dma) and op.inc:
                    c += 1
                    op.count = c
        handles = {"pe": block.tensor, "act": block.scalar, "dve": block.vector,
                   "pool": block.gpsimd, "sp": block.sync}

        def make(e):
            ops = self.ops[e]

            def body(engh):
                waited = {}

                def wait(key, sem, val):
                    if val <= 0:
                        return
                    if waited.get(key, 0) >= val:
                        return
                    waited[key] = val
                    engh.wait_ge(sem, val)

                def waits_of(op):
                    for p in op.deps:
                        if p.dma:
                            wait(("d", p.sem), dma_sems[p.sem], p.semval)
                        else:
                            wait(("e", p.eng), eng_sems[p.eng], p.count)
                    if op.dma:
                        wait(("d", op.sem), dma_sems[op.sem], op.semprev)

                for idx, op in enumerate(ops):
                    if HOIST and op.batch is not None and (idx == 0 or ops[idx - 1].batch != op.batch):
                        j = idx
                        while j < len(ops) and ops[j].batch == op.batch:
                            waits_of(ops[j])
                            j += 1
                    waits_of(op)
                    if op.fn is None:
                        continue
                    ins = op.fn(engh)
                    if op.dma:
                        ins.then_inc(dma_sems[op.sem], 16)
                    elif op.inc:
                        ins.then_inc(eng_sems[e], 1)
            return body

        for e in self.ENGS:
            handles[e](make(e))


def build_program(n_seq=SEQ_PER_CORE, layers=(0, 1), final_norm=True, stop_after=None, debug=False):
    nc = bass.Bass("TRN2", target_bir_lowering=False)
    S = Sched()

    def din(name, shape, dt=F32):
        return nc.dram_tensor(name, list(shape), dt, kind="ExternalInput").ap()

    x_d = din("x", [n_seq, T, DM])
    w_in_d = din("w_in", [DEPTH, DM, IN_COLS])
    w_bra_d = din("w_br_a", [DEPTH, 256, DM])
    w_brb_d = din("w_br_b", [DEPTH, 384, DM])
    w_brc_d = din("w_br_c", [DEPTH, 384, DM])
    w_out_d = din("w_out", [DEPTH, DM, DM])
    gcol_d = din("gcol", [128, DEPTH * KC])
    bfb_d = din("bfb", [128, DEPTH * 4])
    snk_d = din("snk", [128, DEPTH * 6])
    fgb_d = din("fgb", [128, DM])
    ident_d = din("ident", [128, 128])
    masks_d = din("masks", [128, NMASK * 128])
    trio_d = din("trio", [128, 256])
    cos_d = din("cos_t", [128, NT * 32])
    sin_d = din("sin_t", [128, NT * 32])
    out_d = nc.dram_tensor("out", [n_seq, T, DM], F32, kind="ExternalOutput").ap()
    xs_d = nc.dram_tensor("xs_scratch", [n_seq, T, DM], F32, kind="Internal").ap()
    w_in_b = nc.dram_tensor("w_in_bf", [DEPTH, DM, IN_COLS], BF16, kind="Internal").ap()
    w_bra_b = nc.dram_tensor("w_bra_bf", [DEPTH, 256, DM], BF16, kind="Internal").ap()
    w_brb_b = nc.dram_tensor("w_brb_bf", [DEPTH, 384, DM], BF16, kind="Internal").ap()
    w_brc_b = nc.dram_tensor("w_brc_bf", [DEPTH, 384, DM], BF16, kind="Internal").ap()
    w_out_b = nc.dram_tensor("w_out_bf", [DEPTH, DM, DM], BF16, kind="Internal").ap()
    dbg_d = None
    if debug:
        dbg_d = nc.dram_tensor("dbg", [128, 8 * T], BF16, kind="ExternalOutput").ap()

    es = ExitStack()
    with es:
        def sb(name, shape, dt):
            return es.enter_context(nc.sbuf_tensor("s_" + name, list(shape), dt))

        def ps(name, shape, dt):
            return es.enter_context(nc.psum_tensor("p_" + name, list(shape), dt))

        hT = sb("hT", [128, KC, T], BF16)
        REG = sb("REG", [128, 21504], BF16)
        YT = sb("YT", [128, 8, T], BF16)
        WB_ = [sb("W0", [128, KC * 800], BF16), sb("W1", [128, KC * 800], BF16)]
        WS_ = [sb("WS0", [128, KC, 4, 128], BF16), sb("WS1", [128, KC, 4, 128], BF16)]
        xt = [sb("xt%d" % i, [128, DM], F32) for i in range(3)]
        xn = [sb("xn0", [128, DM], BF16), sb("xn1", [128, DM], BF16)]
        Pb = [sb("P%d" % i, [128, 512], BF16) for i in range(6)]
        t1 = [sb("t1_%d" % i, [128, 384], F32) for i in range(2)]
        t2 = [sb("t2_%d" % i, [128, 384], F32) for i in range(2)]
        rp = [sb("rp_%d" % i, [128, 384], BF16) for i in range(2)]
        sgb = [sb("sg%d" % i, [128, 512], BF16) for i in range(2)]
        acc = sb("acc", [128, 512], F32)
        tmpf = [sb("tmpf%d" % i, [128, 512], F32) for i in range(2)]
        rdn = [sb("rdn%d" % i, [128, 512], F32) for i in range(2)]
        masks = sb("masks", [128, NMASK, 128], BF16)
        ident = sb("ident", [128, 128], BF16)
        trio = sb("trio", [128, 2, 128], F32)
        cos_t = sb("cos_t", [128, NT, 32], F32)
        sin_t = sb("sin_t", [128, NT, 32], F32)
        gcol = sb("gcol", [128, DEPTH, KC], F32)
        bfb = sb("bfb", [128, DEPTH, 4], F32)
        snk = sb("snk", [128, DEPTH * 6], F32)
        esk = sb("esk", [128, DEPTH * 6], F32)
        fgb = sb("fgb", [128, DM], F32)
        biasT = sb("biasT", [128, NT, NT, 4], F32)
        fat = sb("fat", [128, NT, 4], F32)
        spall = sb("spall", [128, NT, 4], F32)
        call = sb("call", [128, NT, 4], F32)
        eall = sb("eall", [128, NT, 4], F32)
        emid = sb("emid", [128, NT, 4], F32)
        stat = sb("stat", [128, 8], F32)

        bankJ = [ps("J0", [128, 512], F32), ps("J1", [128, 512], F32)]
        bankS = [ps("S0", [128, 512], F32), ps("S1", [128, 512], F32)]
        bankO = [ps("O0", [128, 512], F32), ps("O1", [128, 512], F32)]
        bankT = [ps("T0", [128, 1024], BF16), ps("T1", [128, 1024], BF16)]

        ctr = {"J": 0, "T": 0, "G": 0, "O": 0}
        SB4 = [bankS[0], bankS[1], bankJ[0], bankJ[1]]
        SR4 = ["S0", "S1", "J0", "J1"]
        NSB = 6
        SB6 = [bankS[0][:, :], bankS[1][:, :], bankJ[0][:, :], bankJ[1][:, :],
               bankT[0][:, :].bitcast(F32), bankT[1][:, :].bitcast(F32)]
        SR6 = ["S0", "S1", "J0", "J1", "T0", "T1"]

        def nextJ():
            k = ctr["J"] % 2
            ctr["J"] += 1
            return bankJ[k], "J%d" % k

        def nextT():
            k = ctr["T"] % 2
            ctr["T"] += 1
            return bankT[k], "T%d" % k

        def nextG():
            k = ctr["G"] % 4
            ctr["G"] += 1
            return ([bankJ[0], bankJ[1], bankS[0], bankS[1]][k], ["J0", "J1", "S0", "S1"][k])

        def PE(fn, r=(), w=(), batch=None):
            return S.add("pe", fn, r, w, batch=batch)

        def ACT(fn, r=(), w=()):
            return S.add("act", fn, r, w)

        def DVE(fn, r=(), w=()):
            return S.add("dve", fn, r, w)

        def POOL(fn, r=(), w=()):
            return S.add("pool", fn, r, w)

        def DMA(q, out, in_, r=(), w=()):
            return S.add(q, lambda e, o=out, i=in_: e.dma_start(out=o, in_=i), r, w, dma=True)

        def mm(out, lhsT, rhs, start, stop, r, w, batch=None, sgc=False):
            PE(lambda e, o=out, l=lhsT, rr=rhs, s0=start, s1=stop, g_=sgc: e.matmul(
                o, lhsT=l, rhs=rr, start=s0, stop=s1, skip_group_check=g_), r, w, batch=batch)

        def bc(ap, shape):
            return ap.to_broadcast(list(shape))

        DMA("pool", ident[:], ident_d, w=["ident"])
        DMA("pool", masks[:], masks_d.rearrange("p (m c) -> p m c", m=NMASK), w=["masks"])
        DMA("sp", trio[:], trio_d.rearrange("p (m c) -> p m c", m=2), w=["trio"])
        DMA("sp", cos_t[:], cos_d.rearrange("p (t c) -> p t c", t=NT), w=["cos"])
        DMA("sp", sin_t[:], sin_d.rearrange("p (t c) -> p t c", t=NT), w=["sin"])
        DMA("sp", gcol[:], gcol_d.rearrange("p (l k) -> p l k", l=DEPTH), w=["gcol"])
        DMA("sp", bfb[:], bfb_d.rearrange("p (l k) -> p l k", l=DEPTH), w=["bfb"])
        DMA("sp", snk[:], snk_d, w=["snk"])
        DMA("sp", fgb[:], fgb_d, w=["fgb"])
        ACT(lambda e: e.activation(out=esk[:], in_=snk[:], func=AF.Exp), r=["snk"], w=["esk"])

        WGROUPS = [(0, 772), (772, 1028), (1028, 2180), (2180, 2564), (2564, 3204), (3204, 3588), (3588, 6660)]

        def wgroup(c0):
            for gi, (a_, b_) in enumerate(WGROUPS):
                if a_ <= c0 < b_:
                    return gi
            raise ValueError

        for l_ in layers:
            for gi in (0, 2, 4):
                a_, b_ = WGROUPS[gi]
                DMA("pool", w_in_b[l_][:, a_:b_], w_in_d[l_][:, a_:b_], w=[("wdram", l_, gi)])
            DMA("pool", w_out_b[l_], w_out_d[l_], w=[("wdram", l_, "out")])
            for gi in (1, 3, 5):
                a_, b_ = WGROUPS[gi]
                DMA("pool", w_in_b[l_][:, a_:b_], w_in_d[l_][:, a_:b_], w=[("wdram", l_, gi)])
            DMA("pool", w_bra_b[l_], w_bra_d[l_], w=[("wdram", l_, "br")])
            DMA("pool", w_brb_b[l_], w_brb_d[l_], w=[("wdram", l_, "br")])
            DMA("pool", w_brc_b[l_], w_brc_d[l_], w=[("wdram", l_, "br")])
            for q_ in range(4):
                a_ = 3588 + q_ * 768
                DMA("pool", w_in_b[l_][:, a_:a_ + 768], w_in_d[l_][:, a_:a_ + 768], w=[("wdram", l_, 6, q_)])

        def wslot_view(k, ncols):
            return WB_[k][:, 0:KC * ncols].rearrange("p (k c) -> p k c", k=KC)

        def load_w_in(k, l, c0, ncols):
            v = wslot_view(k, ncols)
            src = w_in_b[l].rearrange("(k p) c -> p k c", p=128)[:, :, c0:c0 + ncols]
            DMA("sp", v, src, r=[("wdram", l, wgroup(c0))], w=["W%d" % k])
            return v

        def load_w_out(k, l, half):
            v = wslot_view(k, 512)
            src = w_out_b[l].rearrange("(k p) c -> p k c", p=128)[:, :, half * 512:(half + 1) * 512]
            DMA("sp", v, src, r=[("wdram", l, "out")], w=["W%d" % k])
            return (v, "W%d" % k)

        def ws_dma(l, job):
            if job >= 16:
                return
            slot = job % 2
            wres = "WS%d" % slot
            wv = w_in_b[l].rearrange("(k p) c -> p k c", p=128)
            if job < 8:
                ct = job
                zc = (O_ZA + ct * 128) if ct < 2 else ((O_ZB + (ct - 2) * 128) if ct < 5 else (O_ZC + (ct - 5) * 128))
                DMA("sp", WS_[slot][:, :, 0, :], wv[:, :, zc:zc + 128], r=[("wdram", l, wgroup(zc))], w=[wres])
            else:
                dt = job - 8
                for b_, goff in enumerate((O_GA, O_GB, O_GC)):
                    c0_ = goff + dt * 128
                    DMA("sp", WS_[slot][:, :, b_, :], wv[:, :, c0_:c0_ + 128],
                        r=[("wdram", l, 6, (c0_ - 3588) // 768)], w=[wres])
                DMA("sp", WS_[slot][:, 0:2, 3, :],
                    w_bra_b[l].rearrange("(k p) d -> p k d", p=128)[:, :, dt * 128:(dt + 1) * 128],
                    r=[("wdram", l, "br")], w=[wres])
                DMA("sp", WS_[slot][:, 2:5, 3, :],
                    w_brb_b[l].rearrange("(k p) d -> p k d", p=128)[:, :, dt * 128:(dt + 1) * 128],
                    r=[("wdram", l, "br")], w=[wres])
                DMA("sp", WS_[slot][:, 5:8, 3, :],
                    w_brc_b[l].rearrange("(k p) d -> p k d", p=128)[:, :, dt * 128:(dt + 1) * 128],
                    r=[("wdram", l, "br")], w=[wres])

        def hT_res(c):
            return [("hT", 4 * c + j) for j in range(4)]

        def reg_view(off, shape):
            n = int(np.prod(shape))
            v = REG[:, off:off + n]
            if len(shape) == 1:
                return v
            if len(shape) == 2:
                return v.rearrange("p (a b) -> p a b", a=shape[0])
            if len(shape) == 3:
                return v.rearrange("p (a b c) -> p a b c", a=shape[0], b=shape[1])
            raise ValueError

        def claim_region():
            DVE(lambda e: e.engine_nop(), r=(), w=["REGown"])

        RO = ["REGown"]

        def rms_stats(src_ap, slot, res_src):
            ACT(lambda e, s=slot: e.activation(out=xn[s][:], in_=src_ap, func=AF.Square,
                                               accum_out=stat[:, s:s + 1]),
                r=[res_src], w=["xn%d" % slot, ("stat", slot)])
            ACT(lambda e, s=slot: e.activation(out=stat[:, 2 + s:3 + s], in_=stat[:, s:s + 1], func=AF.Ln,
                                               scale=1.0 / DM, bias=RMS_EPS),
                r=[("stat", slot)], w=[("stat", 2 + slot)])
            ACT(lambda e, s=slot: e.activation(out=stat[:, 4 + s:5 + s], in_=stat[:, 2 + s:3 + s], func=AF.Exp,
                                               scale=-0.5),
                r=[("stat", 2 + slot)], w=[("stat", 4 + slot)])

        def norm_tile(slot, i):
            k = i % 2
            rms_stats(xt[slot][:], k, "xt%d" % slot)
            ACT(lambda e, sl=slot, k=k: e.activation(out=xn[k][:], in_=xt[sl][:], func=AF.Copy,
                                                     scale=stat[:, 4 + k:5 + k]),
                r=["xt%d" % slot, ("stat", 4 + k)], w=["xn%d" % k])

        def transpose_tile(i, l):
            k = i % 2
            tb, tres = nextT()
            for kc in range(KC):
                PE(lambda e, tb=tb, k=k, kc=kc: e.transpose(tb[:, kc * 128:(kc + 1) * 128],
                                                            xn[k][:, kc * 128:(kc + 1) * 128], ident[:, :]),
                   r=["xn%d" % k, "ident"], w=[tres])
            DVE(lambda e, tb=tb, i=i: e.tensor_tensor(
                    out=hT[:, :, i * 128:(i + 1) * 128],
                    in0=tb[:, :].rearrange("p (k c) -> p k c", k=KC),
                    in1=bc(gcol[:, l, :].unsqueeze(2), [128, KC, 128]), op=ALU.mult),
                r=["gcol"], w=[tres, ("hT", i)])

        def norm_transpose_tile(slot, i, l):
            norm_tile(slot, i)
            transpose_tile(i, l)

        def phase0(s, l, xsrc, xsrc_res, after_first_loads=None):
            def ld(i):
                if i < NT:
                    DMA("sp", xt[i % 3][:], xsrc[s, i * 128:(i + 1) * 128, :], r=[(xsrc_res, s, i)],
                        w=["xt%d" % (i % 3)])
            ld(0)
            ld(1)
            if after_first_loads is not None:
                after_first_loads()
            for i in range(NT):
                ld(i + 2)
                norm_transpose_tile(i % 3, i, l)

        def write_V(VV, i, pj, pres, npairs, col0=0):
            src = pj[:, col0:col0 + npairs * 128].rearrange("p (a b c) -> p a b c", a=npairs, b=2)
            ACT(lambda e: e.activation(out=VV[:, i, :, 0:64], in_=src[:, :, 0, :], func=AF.Copy),
                r=RO, w=[pres, ("V", i)])
            ACT(lambda e: e.activation(out=VV[:, i, :, 128:192], in_=src[:, :, 1, :], func=AF.Copy),
                r=RO, w=[pres, ("V", i)])

        def set_ones(VV):
            DVE(lambda e: e.memset(VV[:, :, :, 64:128], 1.0), r=RO, w=[("V", i) for i in range(NT)])

        def rope(pj, pres, nh, i, k, perm=None):
            n = nh * 64
            u = pj[:, 0:n].rearrange("p (h t d) -> p h t d", h=nh, t=2)
            a1 = t1[k][:, 0:n].rearrange("p (h t d) -> p h t d", h=nh, t=2)
            a2 = t2[k][:, 0:n].rearrange("p (h t d) -> p h t d", h=nh, t=2)
            cosb = bc(cos_t[:, i, :].unsqueeze(1).unsqueeze(1), [128, nh, 2, 32])
            sinb = bc(sin_t[:, i, :].unsqueeze(1), [128, nh, 32])
            DVE(lambda e: e.tensor_tensor(out=a1, in0=u, in1=cosb, op=ALU.mult), r=["cos"], w=[pres, ("t1", k)])
            DVE(lambda e: e.tensor_tensor(out=a2[:, :, 0, :], in0=u[:, :, 1, :], in1=sinb, op=ALU.mult),
                r=["sin"], w=[pres, ("t2", k)])
            DVE(lambda e: e.tensor_tensor(out=a2[:, :, 1, :], in0=u[:, :, 0, :], in1=sinb, op=ALU.mult),
                r=["sin"], w=[pres, ("t2", k)])
            if perm is None:
                o = rp[k][:, 0:n].rearrange("p (h t d) -> p h t d", h=nh, t=2)
                o0, o1 = o[:, :, 0, :], o[:, :, 1, :]
                i10, i11 = a1[:, :, 0, :], a1[:, :, 1, :]
                i20, i21 = a2[:, :, 0, :], a2[:, :, 1, :]
            else:
                o = rp[k][:, 0:n].rearrange("p (s v t d) -> p v s t d", s=3, v=2, t=2)
                a1v = t1[k][:, 0:n].rearrange("p (v s t d) -> p v s t d", v=2, s=3, t=2)
                a2v = t2[k][:, 0:n].rearrange("p (v s t d) -> p v s t d", v=2, s=3, t=2)
                o0, o1 = o[:, :, :, 0, :], o[:, :, :, 1, :]
                i10, i11 = a1v[:, :, :, 0, :], a1v[:, :, :, 1, :]
                i20, i21 = a2v[:, :, :, 0, :], a2v[:, :, :, 1, :]
            POOL(lambda e: e.tensor_tensor(out=o0, in0=i10, in1=i20, op=ALU.subtract),
                 r=[("t1", k), ("t2", k)], w=[("rp", k)])
            POOL(lambda e: e.tensor_tensor(out=o1, in0=i11, in1=i21, op=ALU.add),
                 r=[("t1", k), ("t2", k)], w=[("rp", k)])

        def transpose_to(dst3, i, k, ntr, dres):
            tb, tres = nextT()
            for j in range(ntr):
                PE(lambda e, tb=tb, j=j: e.transpose(tb[:, j * 128:(j + 1) * 128], rp[k][:, j * 128:(j + 1) * 128],
                                                     ident[:, :]),
                   r=[("rp", k), "ident"], w=[tres])
            ACT(lambda e, tb=tb: e.activation(out=dst3[:, 0:ntr, i * 128:(i + 1) * 128],
                                              in_=tb[:, 0:ntr * 128].rearrange("p (j c) -> p j c", j=ntr),
                                              func=AF.Copy),
                r=RO, w=[tres, (dres, i)])

        def proj_tok(i, wv, c0, ncols, wres):
            pj, pres = nextG()
            for kc in range(KC):
                mm(pj[:, 0:ncols], hT[:, kc, i * 128:(i + 1) * 128], wv[:, kc, c0:c0 + ncols],
                   kc == 0, kc == KC - 1, [("hT", i), wres], [pres])
            return pj, pres

        OB4 = [bankO[0][:, :], bankO[1][:, :], bankT[0][:, :].bitcast(F32), bankT[1][:, :].bitcast(F32)]
        OR4 = ["O0", "O1", "T0", "T1"]
        pgc = [0]

        def attention(heads, pairing):
            def head_units(hd, slot):
                out = []
                for J in range(NCH):
                    grp = []
                    for i in range(hd["kmin"](J), 4 * J + 4):
                        blks = []
                        for bq in range(4):
                            m = hd["maskfn"](4 * J + bq - i)
                            if m != "skip":
                                blks.append((bq, m))
                        if not blks:
                            continue
                        assert [b_ for b_, _ in blks] == list(range(blks[0][0], blks[0][0] + len(blks)))
                        grp.append((hd, J, i, blks))
                    for n_, u in enumerate(grp):
                        out.append(u + (n_ == 0, n_ == len(grp) - 1, slot, J))
                return out

            units = []
            for (he, ho) in pairing:
                ue, uo = head_units(heads[he], 0), head_units(heads[ho], 1)
                assert len(ue) == len(uo)
                base = pgc[0]
                for x_, y_ in zip(ue, uo):
                    units.append(x_ + (0,))
                    units.append(y_ + (1,))
                pgc[0] += NCH

            def emit_qk(n, batch=None):
                hd, J, i, blks, first, last, slot, _, ob = units[n]
                sbk, sres = SB6[n % 6], SR6[n % 6]
                b0 = blks[0][0]
                ncol = len(blks) * 128
                q0 = J * 512 + b0 * 128
                mm(sbk[:, 0:ncol], hd["kt"](i), hd["qt"](q0, ncol), True, True,
                   RO + [("KT", i)] + [("QT", 4 * J + b_) for b_, _ in blks], [sres], batch=batch)

            def emit_mid(n):
                hd, J, i, blks, first, last, slot, _, ob = units[n]
                sbk, sres = SB6[n % 6], SR6[n % 6]
                pb, pres = Pb[n % 6], "P%d" % (n % 6)
                nb = len(blks)
                ncol = nb * 128
                if hd["bias"] is None:
                    ACT(lambda e: e.activation(out=pb[:, 0:ncol], in_=sbk[:, 0:ncol], func=AF.Exp, scale=0.125),
                        r=[], w=[sres, pres])
                elif ACT_BIAS_UNITS and n % 5 in ACT_BIAS_UNITS:
                    for j, (bq, _) in enumerate(blks):
                        bap = hd["bias"](4 * J + bq, 1, i)
                        ACT(lambda e, j=j, bap=bap: e.activation(out=pb[:, j * 128:(j + 1) * 128],
                                                                 in_=sbk[:, j * 128:(j + 1) * 128],
                                                                 func=AF.Exp, scale=0.125, bias=bap),
                            r=["biasT"], w=[sres, pres])
                else:
                    I0 = 4 * J + blks[0][0]
                    bap = bc(hd["bias"](I0, nb, i).unsqueeze(2), [128, nb, 128])
                    sv = sbk[:, 0:ncol].rearrange("p (b c) -> p b c", b=nb)
                    DVE(lambda e, sv=sv, bap=bap: e.scalar_tensor_tensor(out=sv, in0=sv, scalar=0.125, in1=bap,
                                                                         op0=ALU.mult, op1=ALU.add),
                        r=["biasT"], w=[sres])
                    ACT(lambda e: e.activation(out=pb[:, 0:ncol], in_=sbk[:, 0:ncol], func=AF.Exp),
                        r=[], w=[sres, pres])
                mids = [m for _, m in blks]
                if any(m is not None for m in mids):
                    lo_ = min(j for j in range(nb) if mids[j] is not None)
                    hi_ = max(j for j in range(nb) if mids[j] is not None) + 1
                    sub = mids[lo_:hi_]
                    runs = []
                    if all(m is not None for m in sub) and all(sub[j] == sub[0] + j for j in range(len(sub))):
                        runs.append((lo_, hi_, masks[:, sub[0]:sub[0] + len(sub), :]))
                    elif all(m is not None for m in sub) and all(m == sub[0] for m in sub):
                        runs.append((lo_, hi_, bc(masks[:, sub[0], :].unsqueeze(1), [128, len(sub), 128])))
                    else:
                        j = lo_
                        while j < hi_:
                            m = mids[j]
                            j2 = j
                            while j2 < hi_ and mids[j2] == m:
                                j2 += 1
                            if m is not None:
                                runs.append((j, j2, bc(masks[:, m, :].unsqueeze(1), [128, j2 - j, 128])))
                            j = j2
                    for (ja, jb, mk) in runs:
                        view = pb[:, ja * 128:jb * 128].rearrange("p (b c) -> p b c", b=jb - ja)
                        DVE(lambda e, view=view, mk=mk: e.tensor_tensor(out=view, in0=view, in1=mk, op=ALU.mult),
                            r=["masks"], w=[pres])

            def emit_pv(n, batch=None):
                hd, J, i, blks, first, last, slot, _, ob = units[n]
                pb, pres = Pb[n % 6], "P%d" % (n % 6)
                b0 = blks[0][0]
                ncol = len(blks) * 128
                obk, ores = OB4[ob], OR4[ob]
                mm(obk[:, b0 * 128:b0 * 128 + ncol], hd["v"](i), pb[:, 0:ncol], first, last,
                   RO + [pres, ("V", i)], [ores], batch=batch, sgc=True)
                if last:
                    rd = rdn[J % 2]
                    nr, dr, orr = hd["num_rows"], hd["den_rows"], hd["out_rows"]
                    rres = ("rdn", J % 2, orr.start)
                    if hd.get("dve_recip"):
                        DVE(lambda e: e.reciprocal(out=rd[orr, :], in_=obk[dr, :]), r=[], w=[ores, rres])
                    else:
                        if hd["sink"] is not None:
                            sc = hd["sink"]
                            ACT(lambda e: e.activation(out=rd[orr, :], in_=obk[dr, :], func=AF.Ln,
                                                       bias=esk[dr, sc:sc + 1]),
                                r=["esk"], w=[ores, rres])
                        else:
                            ACT(lambda e: e.activation(out=rd[orr, :], in_=obk[dr, :], func=AF.Ln),
                                r=[], w=[ores, rres])
                        ACT(lambda e: e.activation(out=rd[orr, :], in_=rd[orr, :], func=AF.Exp, scale=-1.0),
                            r=[], w=[rres])
                    ct = hd["ct"]
                    DVE(lambda e: e.tensor_tensor(out=YT[orr, ct, J * 512:(J + 1) * 512], in0=obk[nr, :],
                                                  in1=rd[orr, :], op=ALU.mult),
                        r=[rres], w=[ores, ("YT", ct, J)])

            LOOK, GB = 4, 2
            nu = len(units)
            qn = 0
            for m in range(0, nu, GB):
                tag = ("qk", id(units), m)
                while qn < min(nu, m + GB + LOOK):
                    emit_qk(qn, batch=tag)
                    qn += 1
                for n in range(m, min(nu, m + GB)):
                    emit_mid(n)
                tag = ("pv", id(units), m)
                for n in range(m, min(nu, m + GB)):
                    emit_pv(n, batch=tag)

        LO, HI = slice(0, 64), slice(64, 128)

        def layer(s, l, xsrc, xsrc_res, last_layer, skip_p0=False, next_l=None):
            wpre = {}

            def wprefetch():
                wpre["wa"] = load_w_in(1, l, O_QA, 772)
                wpre["wb1"] = load_w_in(0, l, O_QB, 768)
            if not skip_p0:
                phase0(s, l, xsrc, xsrc_res, after_first_loads=wprefetch)
            else:
                wprefetch()
            wa, wb1 = wpre["wa"], wpre["wb1"]

            claim_region()
            QTa = reg_view(0, [2, T])
            KTa = reg_view(2 * T, [2, T])
            VA = reg_view(4 * T, [NT, 2, 192])
            set_ones(VA)
            for ctile in range(4):
                dst = QTa if ctile < 2 else KTa
                dres = "QT" if ctile < 2 else "KT"
                for c in range(NCH):
                    pj, pres = nextJ()
                    for kc in range(KC):
                        mm(pj[:, :], wa[:, kc, ctile * 128:(ctile + 1) * 128], hT[:, kc, c * 512:(c + 1) * 512],
                           kc == 0, kc == KC - 1, hT_res(c) + ["W1"], [pres])
                    ACT(lambda e, pj=pj, dst=dst, ctile=ctile, c=c: e.activation(
                            out=dst[:, ctile % 2, c * 512:(c + 1) * 512], in_=pj[:, :], func=AF.Copy),
                        r=RO, w=[pres] + [(dres, 4 * c + j) for j in range(4)])
            for i in range(NT):
                pj, pres = proj_tok(i, wa, 512, 260, "W1")
                write_V(VA, i, pj, pres, 2)
                DVE(lambda e, pj=pj, i=i: e.tensor_tensor(out=fat[:, i, :], in0=pj[:, 256:260], in1=bfb[:, l, :],
                                                          op=ALU.add),
                    r=["bfb"], w=[pres, "fat"])
            wb2 = load_w_in(1, l, O_VB, 384)
            ACT(lambda e: e.activation(out=spall[:], in_=fat[:], func=AF.Exp, scale=-1.0), r=["fat"], w=["spall"])
            ACT(lambda e: e.activation(out=spall[:], in_=spall[:], func=AF.Ln, bias=1.0), r=["spall"], w=["spall"])
            pj, pres = nextJ()
            spf = spall[:].rearrange("p t h -> p (t h)")
            mm(pj[:, 0:64], trio[:, 0, :], spf, True, True, ["trio", "spall"], [pres])
            mm(pj[:, 64:128], trio[:, 1, :], spf, True, True, ["trio", "spall"], [pres])
            cb = pj[:, 0:64].rearrange("p (t h) -> p t h", t=NT)
            tot = pj[:, 64:128].rearrange("p (t h) -> p t h", t=NT)
            DVE(lambda e: e.tensor_copy(out=eall[:, 0, :], in_=tot[:, 0, :]), r=[], w=[pres, ("eall", 0)])
            for i in range(1, NT):
                DVE(lambda e, i=i: e.tensor_tensor(out=eall[:, i, :], in0=tot[:, i, :], in1=eall[:, i - 1, :],
                                                   op=ALU.add),
                    r=[("eall", i - 1)], w=[pres, ("eall", i)])
            DVE(lambda e: e.tensor_copy(out=call[:, 0, :], in_=cb[:, 0, :]), r=[], w=[pres, "call"])
            DVE(lambda e: e.tensor_tensor(out=call[:, 1:NT, :], in0=cb[:, 1:NT, :], in1=eall[:, 0:NT - 1, :],
                                          op=ALU.add),
                r=[("eall", i) for i in range(NT)], w=[pres, "call"])
            DVE(lambda e: e.tensor_scalar(out=emid[:, 0, :], in0=eall[:, 0, :], scalar1=0.5, scalar2=None,
                                          op0=ALU.mult),
                r=[("eall", i) for i in range(NT)], w=["emid"])
            DVE(lambda e: e.tensor_tensor(out=emid[:, 1:NT, :], in0=eall[:, 1:NT, :], in1=eall[:, 0:NT - 1, :],
                                          op=ALU.add),
                r=[("eall", i) for i in range(NT)], w=["emid"])
            DVE(lambda e: e.tensor_scalar(out=emid[:, 1:NT, :], in0=emid[:, 1:NT, :], scalar1=0.5, scalar2=None,
                                          op0=ALU.mult),
                r=["emid"], w=["emid"])
            for I in range(NT):
                DVE(lambda e, I=I: e.tensor_tensor(out=biasT[:, I, 0:I + 1, :], in0=call[:, 0:I + 1, :],
                                                   in1=bc(emid[:, I, :].unsqueeze(1), [128, I + 1, 4]),
                                                   op=ALU.subtract),
                    r=["call", "emid"], w=["biasT"])
            heads = []
            for h in range(4):
                p_, hf = h // 2, h % 2
                rows = LO if hf == 0 else HI
                heads.append(dict(
                    qt=lambda q0, n, p_=p_, rows=rows: QTa[rows, p_, q0:q0 + n],
                    kt=lambda i, p_=p_, rows=rows: KTa[rows, p_, i * 128:(i + 1) * 128],
                    v=lambda i, p_=p_, hf=hf: VA[:, i, p_, (0 if hf == 0 else 64):(128 if hf == 0 else 192)],
                    num_rows=rows, den_rows=(HI if hf == 0 else LO), out_rows=rows, ct=p_,
                    maskfn=lambda d: ("skip" if d < 0 else (M_LE if d == 0 else None)),
                    bias=lambda I0, nb, i, h=h: biasT[:, I0:I0 + nb, i, h], sink=None, kmin=lambda J: 0, dve_recip=FOX_DVE_RECIP))
            attention(heads, [(0, 1), (2, 3)])
            if stop_after == "A":
                return

            claim_region()
            QTb = reg_view(0, [3, T])
            KTb = reg_view(3 * T, [3, T])
            VB = reg_view(6 * T, [NT, 3, 192])
            set_ones(VB)
            pend = []

            def flush():
                for a_ in pend:
                    transpose_to(*a_)
                del pend[:]
            for i in range(NT):
                pjq, prq = proj_tok(i, wb1, 0, 384, "W0")
                pjk, prk = proj_tok(i, wb1, 384, 384, "W0")
                pjv, prv = proj_tok(i, wb2, 0, 384, "W1")
                flush()
                rope(pjq, prq, 6, i, 0)
                pend.append((QTb, i, 0, 3, "QT"))
                rope(pjk, prk, 6, i, 1)
                pend.append((KTb, i, 1, 3, "KT"))
                write_V(VB, i, pjv, prv, 3)
            flush()
            wc = load_w_in(0, l, O_QC, 640)
            wo = [load_w_out(1, l, 0)]

            def maskB(d):
                if d < 0:
                    return "skip"
                return M_B0 + min(d, 10)
            heads = []
            for h in range(6):
                p_, hf = h // 2, h % 2
                rows = LO if hf == 0 else HI
                heads.append(dict(
                    qt=lambda q0, n, p_=p_, rows=rows: QTb[rows, p_, q0:q0 + n],
                    kt=lambda i, p_=p_, rows=rows: KTb[rows, p_, i * 128:(i + 1) * 128],
                    v=lambda i, p_=p_, hf=hf: VB[:, i, p_, (0 if hf == 0 else 64):(128 if hf == 0 else 192)],
                    num_rows=rows, den_rows=(HI if hf == 0 else LO), out_rows=rows, ct=2 + p_,
                    maskfn=maskB, bias=None, sink=None, kmin=lambda J: 0))
            attention(heads, [(0, 1), (2, 3), (4, 5)])
            if stop_after == "B":
                return

            claim_region()
            QTc = reg_view(0, [3, T])
            KTc = reg_view(3 * T, [1, T])
            VC = reg_view(4 * T, [NT, 1, 192])
            set_ones(VC)
            for i in range(NT):
                pjq, prq = proj_tok(i, wc, 0, 384, "W0")
                pjk, prk = proj_tok(i, wc, 384, 256, "W0")
                flush()
                rope(pjq, prq, 6, i, 0, perm=True)
                pend.append((QTc, i, 0, 3, "QT"))
                rope(pjk, prk, 2, i, 1)
                pend.append((KTc, i, 1, 1, "KT"))
                write_V(VC, i, pjk, prk, 1, col0=128)
            flush()
            wo.append(load_w_out(0, l, 1))
            ws_dma(l, 0)
            heads = []
            for g in range(6):
                kv, sl = g // 3, g % 3
                rows = LO if kv == 0 else HI
                heads.append(dict(
                    qt=lambda q0, n, sl=sl, rows=rows: QTc[rows, sl, q0:q0 + n],
                    kt=lambda i, rows=rows: KTc[rows, 0, i * 128:(i + 1) * 128],
                    v=lambda i, kv=kv: VC[:, i, 0, (0 if kv == 0 else 64):(128 if kv == 0 else 192)],
                    num_rows=rows, den_rows=(HI if kv == 0 else LO), out_rows=(LO if g % 2 == 0 else HI),
                    ct=5 + g // 2,
                    maskfn=lambda d: (M_LE if d == 0 else (M_GT if d == 1 else "skip")),
                    bias=None, sink=l * 6 + g, kmin=lambda J: max(0, 4 * J - 1)))
            attention(heads, [(0, 3), (1, 4), (2, 5)])
            if stop_after == "C":
                return

            n3 = 0
            for ct in range(8):
                slot = ct % 2
                ws_dma(l, ct + 1)
                for c in range(NCH):
                    pj, pres = nextJ()
                    for kc in range(KC):
                        mm(pj[:, :], WS_[slot][:, kc, 0, :], hT[:, kc, c * 512:(c + 1) * 512],
                           kc == 0, kc == KC - 1, hT_res(c) + ["WS%d" % slot], [pres])
                    k = n3 % 2
                    n3 += 1
                    ACT(lambda e, pj=pj, k=k: e.activation(out=sgb[k][:], in_=pj[:, :], func=AF.Silu),
                        r=[], w=[pres, ("sg", k)])
                    POOL(lambda e, ct=ct, c=c, k=k: e.tensor_tensor(out=YT[:, ct, c * 512:(c + 1) * 512],
                                                                    in0=YT[:, ct, c * 512:(c + 1) * 512],
                                                                    in1=sgb[k][:], op=ALU.mult),
                         r=[("sg", k)], w=[("YT", ct, c)])
            if stop_after == "P3a":
                return

            claim_region()
            MT = reg_view(0, [8, T])
            br_ct = [(0, 2), (2, 5), (5, 8)]
            for dt in range(8):
                slot = dt % 2
                wres = "WS%d" % slot
                ws_dma(l, 8 + dt + 1)
                for c in range(NCH):
                    for b_ in range(3):
                        gb_, gres = nextG()
                        for kc in range(KC):
                            mm(gb_[:, :], WS_[slot][:, kc, b_, :], hT[:, kc, c * 512:(c + 1) * 512],
                               kc == 0, kc == KC - 1, hT_res(c) + [wres], [gres])
                        k = n3 % 2
                        n3 += 1
                        ACT(lambda e, gb_=gb_, k=k: e.activation(out=sgb[k][:], in_=gb_[:, :], func=AF.Sigmoid),
                            r=[], w=[gres, ("sg", k)])
                        pb_, pbres = nextG()
                        c0, c1 = br_ct[b_]
                        for ct in range(c0, c1):
                            mm(pb_[:, :], WS_[slot][:, ct, 3, :], YT[:, ct, c * 512:(c + 1) * 512],
                               ct == c0, ct == c1 - 1, [("YT", ct, c), wres], [pbres])
                        if b_ == 0:
                            DVE(lambda e, pb_=pb_, k=k: e.tensor_tensor(out=acc[:], in0=pb_[:, :], in1=sgb[k][:],
                                                                        op=ALU.mult),
                                r=[("sg", k)], w=[pbres, "acc"])
                        else:
                            tk = b_ - 1
                            DVE(lambda e, pb_=pb_, k=k, tk=tk: e.tensor_tensor(out=tmpf[tk][:], in0=pb_[:, :],
                                                                               in1=sgb[k][:], op=ALU.mult),
                                r=[("sg", k)], w=[pbres, ("tmpf", tk)])
                            if b_ == 1:
                                POOL(lambda e, tk=tk: e.tensor_tensor(out=acc[:], in0=acc[:], in1=tmpf[tk][:],
                                                                      op=ALU.add),
                                     r=[("tmpf", tk)], w=["acc"])
                            else:
                                POOL(lambda e, tk=tk, dt=dt, c=c: e.tensor_tensor(
                                        out=MT[:, dt, c * 512:(c + 1) * 512], in0=acc[:], in1=tmpf[tk][:],
                                        op=ALU.add),
                                     r=RO + [("tmpf", tk), "acc"], w=[("MT", dt, c)])
            if stop_after == "P3b":
                return

            def ld3(i):
                if i < NT:
                    DMA("sp", xt[i % 3][:], xsrc[s, i * 128:(i + 1) * 128, :], r=[(xsrc_res, s, i)],
                        w=["xt%d" % (i % 3)])
            for i in range(NT):
                slot = i % 3
                k = i % 2
                xres = "xt%d" % slot
                if i == 0:
                    ld3(0)
                    ld3(1)
                ld3(i + 2)
                for half in range(2):
                    pj, pres = nextG()
                    wv, wres = wo[half]
                    for dc in range(KC):
                        mm(pj[:, :], MT[:, dc, i * 128:(i + 1) * 128], wv[:, dc, :], dc == 0, dc == KC - 1,
                           RO + [("MT", dc, i // 4), wres], [pres])
                    DVE(lambda e, pj=pj, slot=slot, half=half: e.tensor_tensor(
                            out=xt[slot][:, half * 512:(half + 1) * 512], in0=pj[:, :],
                            in1=xt[slot][:, half * 512:(half + 1) * 512], op=ALU.add),
                        r=[], w=[pres, xres])
                if not last_layer:
                    DMA("sp", xs_d[s, i * 128:(i + 1) * 128, :], xt[slot][:], r=[xres], w=[("xs", s, i)])
                    if next_l is not None:
                        norm_tile(slot, i)
                        if i > 0:
                            transpose_tile(i - 1, next_l)
                        if i == NT - 1:
                            transpose_tile(i, next_l)
                else:
                    if final_norm:
                        rms_stats(xt[slot][:], k, xres)
                        DVE(lambda e, slot=slot, k=k: e.scalar_tensor_tensor(
                                out=xt[slot][:], in0=xt[slot][:], scalar=stat[:, 4 + k:5 + k], in1=fgb[:],
                                op0=ALU.mult, op1=ALU.mult),
                            r=[("stat", 4 + k), "fgb"], w=[xres])
                    DMA("sp", out_d[s, i * 128:(i + 1) * 128, :], xt[slot][:], r=[xres], w=[("out", s, i)])

        for s in range(n_seq):
            for li, l in enumerate(layers):
                first = li == 0
                lastl = li == len(layers) - 1
                layer(s, l, x_d if first else xs_d, "xin" if first else "xs", lastl,
                      skip_p0=(not first) and stop_after is None,
                      next_l=(None if (lastl or stop_after is not None) else layers[li + 1]))

        if debug:
            DMA("sp", dbg_d, YT[:].rearrange("p a b -> p (a b)"),
                r=[("YT", ct, c) for ct in range(8) for c in range(NCH)], w=["dbg"])
        fin_reads = ["dbg"] if debug else []
        for s in range(n_seq):
            for i in range(NT):
                fin_reads.append(("out", s, i))
                fin_reads.append(("xs", s, i))
        S.add("sp", None, fin_reads, [])

        eng_sems = {}
        for e in Sched.ENGS:
            eng_sems[e] = es.enter_context(nc.semaphore("sem_" + e))
        dma_sems = [es.enter_context(nc.semaphore("dsem%d" % k)) for k in range(Sched.N_DMA_SEMS)]
        block = es.enter_context(nc.Block())
        S.emit(nc, block, eng_sems, dma_sems)
    return nc, S


def host_constants():
    k = np.arange(128)[:, None]
    q = np.arange(128)[None, :]
    le = (k <= q).astype(np.float32)
    ge = (k >= q).astype(np.float32)
    gt = (k > q).astype(np.float32)
    m4 = (((q - k) % 4) == 0).astype(np.float32)
    m16 = (((q - k) % 16) == 0).astype(np.float32)
    b5 = m16
    masks = np.stack([le, gt, le * (1 + m4 + m16), ge + m4 + m16, m4 + m16, m4 + m16, ge * m4 + m16,
                      b5, b5, b5, b5, b5, b5], axis=1)
    assert masks.shape[1] == NMASK
    masks = np.ascontiguousarray(masks.reshape(128, NMASK * 128).astype(np.float32))
    ident = np.eye(128, dtype=np.float32)
    tri = (k <= q).astype(np.float32)
    ones = np.ones((128, 128), np.float32)
    trio = np.ascontiguousarray(np.concatenate([tri, ones], axis=1))
    pos = np.arange(T, dtype=np.float32)
    inv = (np.float32(10000.0) ** (-(np.arange(0, 64, 2, dtype=np.float32)) / np.float32(64))).astype(np.float32)
    ang = (pos[:, None] * inv[None, :]).astype(np.float32)
    cos = np.cos(ang).astype(np.float32).reshape(NT, 128, 32).transpose(1, 0, 2).reshape(128, NT * 32)
    sin = np.sin(ang).astype(np.float32).reshape(NT, 128, 32).transpose(1, 0, 2).reshape(128, NT * 32)
    return dict(ident=ident, masks=masks, trio=trio, cos_t=np.ascontiguousarray(cos), sin_t=np.ascontiguousarray(sin))


def make_in_maps(x_shards, norm_g, w_in, b_forget, sinks, w_br_a, w_br_b, w_br_c, w_out, final_norm_g):
    consts = host_constants()
    f = lambda a: np.ascontiguousarray(np.asarray(a, dtype=np.float32))
    gcol = f(np.asarray(norm_g).reshape(DEPTH, KC, 128).transpose(2, 0, 1).reshape(128, DEPTH * KC))
    bfb = f(np.broadcast_to(np.asarray(b_forget).reshape(1, DEPTH * 4), (128, DEPTH * 4)))
    snk = f(np.broadcast_to(np.asarray(sinks).reshape(1, DEPTH * 6), (128, DEPTH * 6)))
    fgb = f(np.broadcast_to(np.asarray(final_norm_g).reshape(1, DM), (128, DM)))
    shared = dict(w_in=f(w_in), w_br_a=f(w_br_a), w_br_b=f(w_br_b), w_br_c=f(w_br_c), w_out=f(w_out),
                  gcol=gcol, bfb=bfb, snk=snk, fgb=fgb, **consts)
    return [dict(x=f(xs), **shared) for xs in x_shards]


_PROGRAM = None


def kernel(x, norm_g, w_in, b_forget, sinks, w_br_a, w_br_b, w_br_c, w_out, final_norm_g):
    global _PROGRAM
    x = np.asarray(x, dtype=np.float32)
    shards = [x[c * SEQ_PER_CORE:(c + 1) * SEQ_PER_CORE] for c in range(NCORES)]
    if _PROGRAM is None:
        _PROGRAM = build_program()[0]
    in_maps = make_in_maps(shards, norm_g, w_in, b_forget, sinks, w_br_a, w_br_b, w_br_c, w_out, final_norm_g)
    res = run_bass_kernel_spmd(_PROGRAM, in_maps, core_ids=list(range(NCORES)))
    out = np.concatenate([np.asarray(r["out"], dtype=np.float32) for r in res.results], axis=0)
    return out
```
